# Optimizing a Trainium2 kernel written in Bass

```python
import jax, jax.numpy as jnp
from jax import lax
import numpy as np

D_MODEL = 1024
BATCH = 8
SEQ = 2048
DEPTH = 1
DEC_BATCH = 128
DEC_SEQ = 8
PAST_LEN = 16384
PAGE_SIZE = 128

MIX_DIM = D_MODEL
CONV_DIM = MIX_DIM // 2
CONV_HEADS = 8
CONV_W = 3
POOL_DIM = MIX_DIM - CONV_DIM
POOL_WINDOWS = (2, 4, 8, 16)
POOL_GROUPS = len(POOL_WINDOWS)
POOL_GDIM = POOL_DIM // POOL_GROUPS
POOL_HIST = max(POOL_WINDOWS) - 1
IN_DIM = 3 * CONV_DIM + POOL_DIM
D_FF = 2816
EPS = 1e-6

kernel_name = "hybrid_conv_pool_macaron_decode_step"


def rmsnorm(x, g):
    xf = x.astype(jnp.float32)
    inv = lax.rsqrt(jnp.mean(xf * xf, axis=-1, keepdims=True) + EPS)
    return (xf * inv).astype(x.dtype) * g


def swiglu(h, w_gate, w_up, w_down):
    return (jax.nn.silu(h @ w_gate) * (h @ w_up)) @ w_down


def short_conv(u, hist, conv_w):
    T = u.shape[1]
    ext = jnp.concatenate([hist, u], axis=1)
    out = ext[:, 0:T] * conv_w[0]
    for k in range(1, CONV_W):
        out = out + ext[:, k:k + T] * conv_w[k]
    return out, ext[:, -(CONV_W - 1):]


def multi_scale_pool(p, hist, p0, pool_w, pool_scale):
    Bn, T, _ = p.shape
    ext = jnp.concatenate([hist, p], axis=1)
    extf = ext.astype(jnp.float32)
    cs = jnp.concatenate([jnp.zeros((Bn, 1, POOL_DIM), jnp.float32), jnp.cumsum(extf, axis=1)], axis=1)
    pos = p0 + jnp.arange(T, dtype=jnp.int32)
    outs = []
    for g, w in enumerate(POOL_WINDOWS):
        sl = slice(g * POOL_GDIM, (g + 1) * POOL_GDIM)
        wsum = cs[:, POOL_HIST + 1:POOL_HIST + 1 + T, sl] - cs[:, POOL_HIST + 1 - w:POOL_HIST + 1 - w + T, sl]
        cnt = jnp.minimum(pos + 1, w).astype(jnp.float32)[None, :, None]
        d = (wsum / cnt - extf[:, POOL_HIST:, sl]).astype(p.dtype)
        outs.append(jnp.einsum('btc,cd->btd', d, pool_w[g]))
    y = jnp.concatenate(outs, axis=-1) * pool_scale
    return y, ext[:, -POOL_HIST:]


def layer_step(x, conv_hist, pool_hist, p0,
               norm_ffn1, ffn1_gate, ffn1_up, ffn1_down,
               norm_mix, w_in, conv_w, pool_w, pool_scale, w_out,
               norm_ffn2, ffn2_gate, ffn2_up, ffn2_down, norm_final):
    x = x + 0.5 * swiglu(rmsnorm(x, norm_ffn1), ffn1_gate, ffn1_up, ffn1_down)
    h = rmsnorm(x, norm_mix)
    proj = h @ w_in
    gb = proj[..., 0:CONV_DIM]
    gc = proj[..., CONV_DIM:2 * CONV_DIM]
    u = proj[..., 2 * CONV_DIM:3 * CONV_DIM]
    p = proj[..., 3 * CONV_DIM:]
    conv_out, new_conv = short_conv(gc * u, conv_hist, conv_w)
    y_conv = gb * conv_out
    y_pool, new_pool = multi_scale_pool(p, pool_hist, p0, pool_w, pool_scale)
    x = x + jnp.concatenate([y_conv, y_pool], axis=-1) @ w_out
    x = x + 0.5 * swiglu(rmsnorm(x, norm_ffn2), ffn2_gate, ffn2_up, ffn2_down)
    return rmsnorm(x, norm_final), new_conv, new_pool


def setup_inputs(seed: int = 0) -> dict:
    key = jax.random.key(seed)
    ks = jax.random.split(key, 24)
    f = jnp.float32
    nrm = lambda k, shape, s: jax.random.normal(k, shape, f) * s
    return {
        "x_prompt": nrm(ks[0], (BATCH, SEQ, D_MODEL), 1.0),
        "x_sample": nrm(ks[1], (DEC_BATCH, DEC_SEQ, D_MODEL), 1.0),
        "state_conv": nrm(ks[2], (DEC_BATCH, CONV_W - 1, CONV_DIM), 0.5),
        "state_pool": nrm(ks[3], (DEC_BATCH, POOL_HIST, POOL_DIM), 1.0),
        "norm_ffn1": 1.0 + nrm(ks[4], (D_MODEL,), 0.05),
        "ffn1_gate": nrm(ks[5], (D_MODEL, D_FF), D_MODEL ** -0.5),
        "ffn1_up": nrm(ks[6], (D_MODEL, D_FF), D_MODEL ** -0.5),
        "ffn1_down": nrm(ks[7], (D_FF, D_MODEL), D_FF ** -0.5),
        "norm_mix": 1.0 + nrm(ks[8], (D_MODEL,), 0.05),
        "w_in": nrm(ks[9], (D_MODEL, IN_DIM), D_MODEL ** -0.5),
        "conv_w": nrm(ks[10], (CONV_W, CONV_DIM), CONV_W ** -0.5),
        "pool_w": nrm(ks[11], (POOL_GROUPS, POOL_GDIM, POOL_GDIM), POOL_GDIM ** -0.5),
        "pool_scale": 1.0 + nrm(ks[12], (POOL_DIM,), 0.05),
        "w_out": nrm(ks[13], (MIX_DIM, D_MODEL), MIX_DIM ** -0.5),
        "norm_ffn2": 1.0 + nrm(ks[14], (D_MODEL,), 0.05),
        "ffn2_gate": nrm(ks[15], (D_MODEL, D_FF), D_MODEL ** -0.5),
        "ffn2_up": nrm(ks[16], (D_MODEL, D_FF), D_MODEL ** -0.5),
        "ffn2_down": nrm(ks[17], (D_FF, D_MODEL), D_FF ** -0.5),
        "norm_final": 1.0 + nrm(ks[18], (D_MODEL,), 0.05),
    }


def reference(x_prompt, x_sample, state_conv, state_pool,
              norm_ffn1, ffn1_gate, ffn1_up, ffn1_down,
              norm_mix, w_in, conv_w, pool_w, pool_scale, w_out,
              norm_ffn2, ffn2_gate, ffn2_up, ffn2_down, norm_final):
    weights = (norm_ffn1, ffn1_gate, ffn1_up, ffn1_down,
               norm_mix, w_in, conv_w, pool_w, pool_scale, w_out,
               norm_ffn2, ffn2_gate, ffn2_up, ffn2_down, norm_final)
    hp, cp, pp = x_prompt, jnp.zeros((x_prompt.shape[0], CONV_W - 1, CONV_DIM), x_prompt.dtype), \
        jnp.zeros((x_prompt.shape[0], POOL_HIST, POOL_DIM), x_prompt.dtype)
    hs, cs_, ps = x_sample, state_conv.astype(x_sample.dtype), state_pool.astype(x_sample.dtype)
    for _ in range(DEPTH):
        hp, cp, pp = layer_step(hp, cp, pp, 0, *weights)
        hs, cs_, ps = layer_step(hs, cs_, ps, PAST_LEN, *weights)
    return (hp, hs, cp, pp, cs_, ps)
```

```python
import numpy as np
import concourse.bass as bass
import concourse.mybir as mybir
from concourse.bass_utils import run_bass_kernel_spmd

F32 = mybir.dt.float32
BF16 = mybir.dt.bfloat16
ALU = mybir.AluOpType
AF = mybir.ActivationFunctionType

NCORES = 8
D = 1024
DFF = 2816
KC = 8
TP = 2048
TS = 128
NT = 17
NTOK = TP + TS
SSEQ = 16
ST = 8
CONVD = 512
POOLD = 512
EPS = 1e-6
WINDOWS = (2, 4, 8, 16)
BLOCKS = [[0, 1, 2, 3], [4, 5, 6, 7], [8, 9, 10, 11], [12, 13, 14, 15], [16, 17, 18], [19, 20, 21]]
NRING = 6
SAME_ENG_SYNC = True


class Op:
    __slots__ = ("eng", "fn", "deps", "eidx", "sig", "sem", "val", "dma", "waits", "gidx", "extra_waits")


class Prog:
    ENGS = ("pe", "act", "dve", "pool", "sp")

    def __init__(self):
        self.order = []
        self.byeng = {e: [] for e in self.ENGS}
        self.res = {}

    def op(self, eng, fn, reads=(), writes=(), dma=False, after=()):
        o = Op()
        o.eng, o.fn, o.dma, o.sig = eng, fn, dma, False
        o.sem = None
        o.val = 0
        o.extra_waits = []
        deps = set()
        for r in reads:
            st = self.res.get(r)
            if st is not None and st[0] is not None:
                deps.add(st[0])
        for w in writes:
            st = self.res.get(w)
            if st is not None:
                if st[0] is not None:
                    deps.add(st[0])
                deps.update(st[1])
        for r in reads:
            st = self.res.setdefault(r, [None, []])
            st[1].append(o)
        for w in writes:
            self.res[w] = [o, []]
        deps.update(after)
        deps.discard(o)
        o.deps = deps
        o.gidx = len(self.order)
        o.eidx = len(self.byeng[eng])
        self.order.append(o)
        self.byeng[eng].append(o)
        return o

    def alias(self, new_ids, old_ids):
        acc = []
        for r in old_ids:
            st = self.res.get(r)
            if st is None:
                continue
            if st[0] is not None:
                acc.append(st[0])
            acc.extend(st[1])
        for n in new_ids:
            st = self.res.setdefault(n, [None, []])
            st[1].extend(acc)

    def finalize(self, nc, sems, dma_sems):
        for o in self.order:
            best = {}
            dmadeps = []
            for d in o.deps:
                if d.dma:
                    dmadeps.append(d)
                    continue
                if (not o.dma) and d.eng == o.eng and (o.eng == "pe" or not SAME_ENG_SYNC):
                    continue
                cur = best.get(d.eng)
                if cur is None or d.eidx > cur.eidx:
                    best[d.eng] = d
            o.waits = list(best.values()) + dmadeps
            for d in o.waits:
                d.sig = True
        cnt = {e: 0 for e in self.ENGS}
        dcnt = {e: 0 for e in self.ENGS}
        duse = {}
        for e in self.ENGS:
            for o in self.byeng[e]:
                if o.dma:
                    pool = dma_sems[e]
                    s = pool[dcnt[e] % len(pool)]
                    dcnt[e] += 1
                    prev = duse.get(s, 0)
                    if prev:
                        o.extra_waits.append((s, prev))
                    o.sem = s
                    o.val = prev + 16
                    duse[s] = o.val
                    o.sig = True
                elif o.sig:
                    cnt[e] += 1
                    o.sem = sems[e]
                    o.val = cnt[e]

    def emit_engine(self, e, eng):
        waited = {}
        for o in self.byeng[e]:
            ws = [(d.sem, d.val) for d in o.waits] + o.extra_waits
            for (s, v) in ws:
                if waited.get(s, 0) < v:
                    eng.wait_ge(s, v)
                    waited[s] = v
            ins = o.fn(eng)
            if o.sig and ins is not None:
                ins.then_inc(o.sem, 16 if o.dma else 1)


def build_nc():
    nc = bass.Bass("TRN2", target_bir_lowering=False)
    P = Prog()

    def din(name, shape):
        return nc.dram_tensor(name, list(shape), F32, kind="ExternalInput").ap()

    def dout(name, shape):
        return nc.dram_tensor(name, list(shape), F32, kind="ExternalOutput").ap()

    xp = din("xp", [TP, D])
    xs = din("xs", [TS, D])
    sc_d = din("sc", [SSEQ * 2, CONVD])
    spl_d = din("spl", [SSEQ * 15, POOLD])
    norm_ffn1 = din("norm_ffn1", [D])
    ffn1_gate = din("ffn1_gate", [D, DFF])
    ffn1_up = din("ffn1_up", [D, DFF])
    ffn1_down = din("ffn1_down", [DFF, D])
    norm_mix = din("norm_mix", [D])
    w_in = din("w_in", [D, 2048])
    conv_w = din("conv_w", [3, CONVD])
    pool_w = din("pool_w", [4, 128, 128])
    pool_scale = din("pool_scale", [POOLD])
    w_out = din("w_out", [D, D])
    norm_ffn2 = din("norm_ffn2", [D])
    ffn2_gate = din("ffn2_gate", [D, DFF])
    ffn2_up = din("ffn2_up", [D, DFF])
    ffn2_down = din("ffn2_down", [DFF, D])
    norm_final = din("norm_final", [D])

    yp = dout("yp", [TP, D])
    ys = dout("ys", [TS, D])
    ncp = dout("ncp", [2, CONVD])
    npp = dout("npp", [15, POOLD])
    ncs = dout("ncs", [SSEQ * 2, CONVD])
    nps = dout("nps", [SSEQ * 15, POOLD])

    xres = nc.alloc_sbuf_tensor("xres", [128, NT, D], F32)
    hT = nc.alloc_sbuf_tensor("hT", [128, KC, NTOK], BF16)
    ring = nc.alloc_sbuf_tensor("ring", [128, NRING, 4096], BF16)
    SCR = nc.alloc_sbuf_tensor("scr", [128, 4096], F32)
    hp = nc.alloc_sbuf_tensor("hp", [128, 3, D], BF16)
    gbc = nc.alloc_sbuf_tensor("gbc", [128, D], F32)
    ost = nc.alloc_sbuf_tensor("ost", [128, D], F32)
    idf = nc.alloc_sbuf_tensor("idf", [128, 128], F32)
    idb = nc.alloc_sbuf_tensor("idb", [128, 128], BF16)
    iot = nc.alloc_sbuf_tensor("iot", [128, 128], F32)
    stt = nc.alloc_sbuf_tensor("stt", [128, 16], F32)
    nhalf = nc.alloc_sbuf_tensor("nhalf", [128, 1], F32)
    deps_t = nc.alloc_sbuf_tensor("deps_t", [128, 1], F32)
    cw = nc.alloc_sbuf_tensor("cw", [128, 12], F32)
    pscale = nc.alloc_sbuf_tensor("pscale", [128, 4], F32)
    rc = nc.alloc_sbuf_tensor("rc", [128, 16], F32)
    poolw = nc.alloc_sbuf_tensor("poolw", [128, 4, 128], BF16)
    pext = nc.alloc_sbuf_tensor("pext", [128, 4, 16 + 256], F32)
    vexts = nc.alloc_sbuf_tensor("vexts", [128, 4, SSEQ * 10], F32)
    pexts = nc.alloc_sbuf_tensor("pexts", [128, 4, SSEQ * 23], F32)
    hp3 = nc.alloc_sbuf_tensor("hp3", [128, D], BF16)
    sc_in = hp3[0:32, :].bitcast(F32)
    sp_in = nc.alloc_sbuf_tensor("sp_in", [120, 2, POOLD], F32)
    stgc = nc.alloc_sbuf_tensor("stgc", [128, 4, 32], F32)
    stgp = nc.alloc_sbuf_tensor("stgp", [128, 240], F32)

    aT = SCR[:, 0:2048].bitcast(BF16).rearrange("p (b j t) -> p b j t", b=2, j=4)
    sg = SCR[:, 2048:3072].rearrange("p (k t) -> p k t", k=2)
    ost2 = SCR[:, 3072:4096]
    mixT = SCR[:, 0:2048].bitcast(BF16).rearrange("p (b c t) -> p b c t", b=2, c=8)
    Csb = SCR[:, 2048:2304]
    tbuf = SCR[:, 2304:2560]
    Sa = SCR[:, 2560:2928]
    Sb = SCR[:, 2928:3296]
    dT = SCR[:, 3296:3808].bitcast(BF16).rearrange("p (g t) -> p g t", g=4)
    Bsb = SCR[:, 3808:4064]
    vext = nc.alloc_sbuf_tensor("vext", [128, 2, 258], F32)
    vhalo = nc.alloc_sbuf_tensor("vhalo", [128, 4, 2], F32)
    Bsb2 = nc.alloc_sbuf_tensor("Bsb2", [128, 256], F32)
    tbuf2 = nc.alloc_sbuf_tensor("tbuf2", [128, 256], F32)
    FFN_SCR_IDS = [("aT", b, j) for b in range(2) for j in range(4)] + [("sg", 0), ("sg", 1)]
    MIX_SCR_IDS = [("mixT", b, c) for b in range(2) for c in range(8)] + [("Csb",), ("Bsb",), ("tbuf",), ("Sa",), ("Sb",)] + \
                  [("dT", gi) for gi in range(4)]

    ps = [nc.alloc_psum_tensor("ps%d" % b, [128, 512], F32) for b in range(8)]

    def ring_gu(s):
        return ring[:, s, :].rearrange("p (k f) -> p k f", k=8)

    def ring_dn(s):
        return ring[:, s, :].rearrange("p (j d) -> p j d", j=4)

    P.op("pool", lambda e: e.iota(iot[:, :], [[1, 128]], base=0, channel_multiplier=-1,
                                  allow_small_or_imprecise_dtypes=True), writes=[("iot",)])
    P.op("dve", lambda e: e.tensor_single_scalar(out=idf[:, :], in_=iot[:, :], scalar=0.0, op=ALU.is_equal),
         reads=[("iot",)], writes=[("idf",)])
    P.op("dve", lambda e: e.tensor_copy(out=idb[:, :], in_=idf[:, :]), reads=[("idf",)], writes=[("idb",)])
    P.op("pool", lambda e: e.memset(nhalf[:, :], -0.5), writes=[("nhalf",)])
    P.op("pool", lambda e: e.memset(deps_t[:, :], float(D) * EPS), writes=[("deps_t",)])
    P.op("pool", lambda e: e.iota(rc[:, :], [[1, 16]], base=1, channel_multiplier=0,
                                  allow_small_or_imprecise_dtypes=True), writes=[("rc",)])
    P.op("dve", lambda e: e.reciprocal(out=rc[:, :], in_=rc[:, :]), reads=[("rc",)], writes=[("rc",)])
    P.op("act", lambda e: e.activation(out=stt[:, 15:16], in_=nhalf[:, :], func=AF.Square),
         reads=[("nhalf",)], writes=[("warm",)])

    slot_next = [0]

    load_after = [()]
    unit_ops = {}

    def load_unit(kind, W, c0, nb, slot=None):
        if slot is None:
            s = slot_next[0] % NRING
            slot_next[0] += 1
        else:
            s = slot
        if kind == "gu":
            src = W.rearrange("(k p) f -> p k f", p=128)[:, :, c0 * 128:(c0 + nb) * 128]
            dst = ring_gu(s)[:, :, 0:nb * 128]
        else:
            src = W[c0 * 128:(c0 + nb) * 128, :].rearrange("(j p) d -> p j d", p=128)
            dst = ring_dn(s)[:, 0:nb, :]
        unit_ops[s] = P.op("pool", lambda e, dst=dst, src=src: e.dma_start(out=dst, in_=src),
                           writes=[("ring", s)], dma=True, after=load_after[0])
        return s

    ffn_w = {1: (ffn1_gate, ffn1_up, ffn1_down), 2: (ffn2_gate, ffn2_up, ffn2_down)}
    ffn_slots = {}

    ffn_part = {}

    def load_ffn_block(f, b, parts="gud", slots=None):
        blk = BLOCKS[b]
        Wg, Wu, Wd = ffn_w[f]
        d = ffn_part.setdefault((f, b), {})
        for i, pch in enumerate("gud"):
            if pch not in parts:
                continue
            W = (Wg, Wu, Wd)[i]
            sl = None if slots is None else slots[i]
            d[pch] = load_unit("gu" if pch != "d" else "dn", W, blk[0], len(blk), slot=sl)
        if len(d) == 3:
            ffn_slots[(f, b)] = (d["g"], d["u"], d["d"])

    mix_slots = {}

    def load_mix_units(names):
        for nm in names:
            if nm in ("C", "u", "B", "p"):
                q = {"B": 0, "C": 1, "u": 2, "p": 3}[nm]
                mix_slots[nm] = load_unit("gu", w_in, q * 4, 4)
            else:
                h = int(nm[1])
                mix_slots[nm] = load_unit("dn", w_out, h * 4, 4)

    def load_first_block(j0, j1, with_down):
        blk = BLOCKS[0]
        gs_, us_, ds_ = 0, 1, 2
        slot_next[0] = 3
        for jj in range(j0, j1):
            for W, s_ in ((ffn1_gate, gs_), (ffn1_up, us_)):
                src = W.rearrange("(k p) f -> p k f", p=128)[:, :, (blk[0] + jj) * 128:(blk[0] + jj + 1) * 128]
                dst = ring_gu(s_)[:, :, jj * 128:(jj + 1) * 128]
                unit_ops[s_] = P.op("pool", lambda e, dst=dst, src=src: e.dma_start(out=dst, in_=src),
                                    writes=[("ringc", s_, jj)], dma=True)
        if with_down:
            load_unit("dn", ffn1_down, blk[0], len(blk), slot=ds_)
        ffn_slots[(1, 0)] = (gs_, us_, ds_)
    load_first_block(0, 2, False)

    xload_ops = []
    xload_after = [()]

    def load_x(tt):
        src = xp[tt * 128:(tt + 1) * 128, :] if tt < 16 else xs[:, :]
        xload_ops.append(P.op("sp", lambda e: e.dma_start(out=xres[:, tt, :], in_=src),
                              writes=[("x", tt, 0), ("x", tt, 1)], dma=True, after=xload_after[0]))

    def load_gbc(g_ap):
        P.op("sp", lambda e: e.dma_start(out=gbc[:, :], in_=g_ap.partition_broadcast(128)),
             writes=[("gbc",)], dma=True)
        P.op("dve", lambda e: e.tensor_scalar(out=gbc[:, :], in0=gbc[:, :], scalar1=float(D) ** 0.5, scalar2=None,
                                              op0=ALU.mult),
             reads=[("gbc",)], writes=[("gbc",)])

    load_gbc(norm_ffn1)
    for tt in range(4):
        load_x(tt)
    gs0, us0, ds0 = ffn_slots[(1, 0)]
    xload_after[0] = (unit_ops[gs0], unit_ops[us0])
    for tt in range(4, 8):
        load_x(tt)
    xload_after[0] = ()

    def load_x_rest():
        xload_after[0] = (unit_ops[ds0],)
        for tt in range(8, NT):
            load_x(tt)
        xload_after[0] = ()
    for i in range(4):
        for k in range(3):
            src = conv_w[k, i * 128:(i + 1) * 128].rearrange("(p o) -> p o", o=1)
            P.op("sp", lambda e, i=i, k=k, src=src: e.dma_start(out=cw[:, i * 3 + k:i * 3 + k + 1], in_=src),
                 writes=[("cw", i * 3 + k)], dma=True)
        src = pool_scale[i * 128:(i + 1) * 128].rearrange("(p o) -> p o", o=1)
        P.op("sp", lambda e, i=i, src=src: e.dma_start(out=pscale[:, i:i + 1], in_=src),
             writes=[("pscale", i)], dma=True)
    P.op("sp", lambda e: e.dma_start(out=sp_in[:, :, :], in_=spl_d.rearrange("(h r) c -> r h c", r=120)),
         writes=[("sp_in",)], dma=True)
    P.op("pool", lambda e: e.dma_start(out=poolw[:, :, :], in_=pool_w.rearrange("g c d -> c g d")),
         writes=[("poolw",)], dma=True)

    hp_cnt = [0]
    hp_of = {}

    def stat(k, j):
        return stt[:, k * 4 + j:k * 4 + j + 1]

    nhp = [4]

    def hpbuf(k):
        return hp[:, k, :] if k < 3 else hp3[:, :]

    def nA1(tt):
        k = hp_cnt[0] % nhp[0]
        hp_cnt[0] += 1
        hp_of[tt] = k
        P.op("act", lambda e: e.activation(out=hpbuf(k), in_=xres[:, tt, :], func=AF.Square,
                                           accum_out=stat(k, 0)),
             reads=[("x", tt, 0), ("x", tt, 1)], writes=[("hp", k), ("st", k)])

    def nA2(tt):
        k = hp_of[tt]
        P.op("pool", lambda e: e.tensor_tensor(out=stat(k, 1), in0=stat(k, 0), in1=deps_t[:, :], op=ALU.add),
             reads=[("st", k), ("deps_t",)], writes=[("st", k)])
        P.op("pool", lambda e: e.tensor_tensor(out=stat(k, 2), in0=stat(k, 1), in1=nhalf[:, :], op=ALU.pow),
             reads=[("st", k), ("nhalf",)], writes=[("st", k)])

    def nA3(tt):
        k = hp_of[tt]
        P.op("dve", lambda e: e.scalar_tensor_tensor(out=hpbuf(k), in0=xres[:, tt, :], scalar=stat(k, 2),
                                                     in1=gbc[:, :], op0=ALU.mult, op1=ALU.mult),
             reads=[("x", tt, 0), ("x", tt, 1), ("st", k), ("gbc",)], writes=[("hp", k)])

    tb_cnt = [0]
    tb_banks = [[7, 6]]

    def norm_B(tt, bank=None):
        k = hp_of[tt]
        if bank is None:
            bank = tb_banks[0][tb_cnt[0] % len(tb_banks[0])]
            tb_cnt[0] += 1
        pst = ps[bank][:, :].bitcast(BF16)

        def pe_fn(e):
            ins = None
            for kc in range(KC):
                ins = e.transpose(out=pst[:, kc * 128:(kc + 1) * 128], in_=hpbuf(k)[:, kc * 128:(kc + 1) * 128],
                                  identity=idb[:, :])
            return ins
        P.op("pe", pe_fn, reads=[("hp", k), ("idb",)], writes=[("ps", bank)])
        P.op("act", lambda e: e.activation(out=hT[:, :, tt * 128:(tt + 1) * 128],
                                           in_=pst.rearrange("p (a b) -> p a b", a=KC), func=AF.Copy),
             reads=[("ps", bank)], writes=[("hT", tt)])

    def norm_seq(tiles, last=None):
        if last is None:
            last = norm_B
        n = len(tiles)
        NHP = nhp[0]
        m = min(NHP, n)
        seq = []
        for step in (nA1, nA2, nA3):
            for t in tiles[:m]:
                seq.append(lambda t=t, step=step: step(t))
        for i in range(n):
            seq.append(lambda t=tiles[i]: last(t))
            if i + NHP < n:
                for step in (nA1, nA2, nA3):
                    seq.append(lambda t=tiles[i + NHP], step=step: step(t))
        return seq

    def norm_tiles(tiles):
        for c in norm_seq(tiles):
            c()

    def merge(a, b):
        out = []
        na, nb_ = len(a), len(b)
        ia = ib = 0
        while ia < na or ib < nb_:
            if ib >= nb_ or (ia < na and ia * nb_ <= ib * na):
                out.append(a[ia])
                ia += 1
            else:
                out.append(b[ib])
                ib += 1
        return out

    FG = [(g * 512, 512, [4 * g + i for i in range(4)]) for g in range(4)] + [(TP, TS, [16])]

    gu_cnt = [0]
    d_cnt = [0]
    d_nb = [3]
    item_cnt = [0]

    def ffn_GU_ops(f, b, g, buf):
        blk = BLOCKS[b]
        tok0, N, tiles = FG[g]
        gs, us, _ = ffn_slots[(f, b)]
        outl = []
        for jj in range(len(blk)):
            def emit(jj=jj):
                c = gu_cnt[0]
                gu_cnt[0] += 1
                gb, ub, sk = c % 2, 2 + c % 2, c % 2
                hreads = [("hT", tt) for tt in tiles]
                if (f, b) == (1, 0):
                    hreads = hreads + [("ringc", gs, jj), ("ringc", us, jj)]

                def mk(slot, bank):
                    def pe_fn(e):
                        ins = None
                        for kc in range(KC):
                            ins = e.matmul(ps[bank][:, 0:N], lhsT=ring_gu(slot)[:, kc, jj * 128:(jj + 1) * 128],
                                           rhs=hT[:, kc, tok0:tok0 + N], start=(kc == 0), stop=(kc == KC - 1))
                        return ins
                    return pe_fn
                P.op("pe", mk(gs, gb), reads=[("ring", gs)] + hreads, writes=[("ps", gb)])
                P.op("pe", mk(us, ub), reads=[("ring", us)] + hreads, writes=[("ps", ub)])
                P.op("act", lambda e: e.activation(out=sg[:, sk, 0:N], in_=ps[gb][:, 0:N], func=AF.Silu),
                     reads=[("ps", gb)], writes=[("sg", sk)])
                P.op("dve", lambda e: e.tensor_tensor(out=aT[:, buf, jj, 0:N], in0=sg[:, sk, 0:N],
                                                      in1=ps[ub][:, 0:N], op=ALU.mult),
                     reads=[("sg", sk), ("ps", ub)], writes=[("aT", buf, jj)])
            outl.append(emit)
        return outl

    def ffn_D_ops(f, b, g, buf):
        blk = BLOCKS[b]
        nb = len(blk)
        tok0, N, tiles = FG[g]
        _, _, ds = ffn_slots[(f, b)]
        outl = []
        for ti, tt in enumerate(tiles):
            for dh in range(2):
                def emit(ti=ti, tt=tt, dh=dh):
                    c = d_cnt[0]
                    d_cnt[0] += 1
                    db = 4 + c % d_nb[0]

                    def pe_fn(e):
                        ins = None
                        for jj in range(nb):
                            ins = e.matmul(ps[db][:, 0:512], lhsT=aT[:, buf, jj, ti * 128:(ti + 1) * 128],
                                           rhs=ring_dn(ds)[:, jj, dh * 512:(dh + 1) * 512],
                                           start=(jj == 0), stop=(jj == nb - 1))
                        return ins
                    P.op("pe", pe_fn, reads=[("ring", ds)] + [("aT", buf, jj) for jj in range(nb)],
                         writes=[("ps", db)])
                    xs_ = xres[:, tt, dh * 512:(dh + 1) * 512]
                    P.op("dve", lambda e: e.scalar_tensor_tensor(out=xs_, in0=ps[db][:, 0:512], scalar=0.5,
                                                                 in1=xs_, op0=ALU.mult, op1=ALU.add),
                         reads=[("ps", db), ("x", tt, dh)], writes=[("x", tt, dh)])
                outl.append(emit)
        return outl

    def interleave(a, b):
        na, nb_ = len(a), len(b)
        ia = ib = 0
        while ia < na or ib < nb_:
            if ia < na:
                a[ia]()
                ia += 1
            while ib < nb_ and (ia >= na or ib * na < ia * nb_):
                b[ib]()
                ib += 1

    ob_cnt = [0]
    cv_cnt = [0]

    def psh(h):
        return ps[h][:, 0:256]

    def mix_geom(g):
        if g == 8:
            return dict(sample=True, tok0=TP, N=TS, tiles=[16], S=SSEQ, T=ST, Hp=15)
        return dict(sample=False, tok0=g * 256, N=256, tiles=[2 * g, 2 * g + 1], S=1, T=256, Hp=16)

    def win_mm(G, nm, col, h):
        slot = mix_slots[nm]
        N, tok0 = G["N"], G["tok0"]

        def pe_fn(e):
            ins = None
            for kc in range(KC):
                ins = e.matmul(psh(h)[:, 0:N], lhsT=ring_gu(slot)[:, kc, col * 128:(col + 1) * 128],
                               rhs=hT[:, kc, tok0:tok0 + N], start=(kc == 0), stop=(kc == KC - 1))
            return ins
        P.op("pe", pe_fn, reads=[("ring", slot)] + [("hT", tt) for tt in G["tiles"]], writes=[("ps", h)])

    pool_ctx = {}

    def mix_pool_win(g):
        G = mix_geom(g)
        sample, N, S, T, Hp = G["sample"], G["N"], G["S"], G["T"], G["Hp"]

        def v3(ap):
            return ap.rearrange("p (s t) -> p s t", s=S)
        PB = (4, 7)
        for pr in range(2):
            bank = PB[pr]
            slot = mix_slots["p"]

            def pe_fn(e, pr=pr, bank=bank, slot=slot):
                ins = None
                for gi in (2 * pr, 2 * pr + 1):
                    for kc in range(KC):
                        ins = e.matmul(ps[bank][:, (gi % 2) * 256:(gi % 2) * 256 + N],
                                       lhsT=ring_gu(slot)[:, kc, gi * 128:(gi + 1) * 128],
                                       rhs=hT[:, kc, G["tok0"]:G["tok0"] + N], start=(kc == 0), stop=(kc == KC - 1))
                return ins
            P.op("pe", pe_fn, reads=[("ring", slot)] + [("hT", tt) for tt in G["tiles"]], writes=[("ps", bank)])
        for gi in range(4):
            w = WINDOWS[gi]
            pbank = PB[gi // 2]
            pcol = (gi % 2) * 256
            if sample:
                pe_ = pexts[:, gi, :].rearrange("p (s e) -> p s e", s=SSEQ)
                pid = ("pexts", gi)
            else:
                pe_ = pext[:, gi, :].rearrange("p (s e) -> p s e", s=1)
                pid = ("pext", gi)
                if g == 0:
                    P.op("dve", lambda e, pe_=pe_: e.memset(pe_[:, :, 0:16], 0.0), writes=[pid])
            L = Hp + T
            P.op("act", lambda e, pe_=pe_, pbank=pbank, pcol=pcol: e.activation(
                out=pe_[:, :, Hp:Hp + T], in_=v3(ps[pbank][:, pcol:pcol + N]), func=AF.Copy),
                reads=[("ps", pbank), pid], writes=[pid])
            sa3 = Sa[:, 0:S * L].rearrange("p (s e) -> p s e", s=S)
            sb3 = Sb[:, 0:S * L].rearrange("p (s e) -> p s e", s=S)
            fin3 = ost[:, gi * 256:gi * 256 + N].rearrange("p (s t) -> p s t", s=S)
            fid = ("F", gi)
            steps = []
            lo, sh = 0, 1
            while sh < w:
                steps.append((lo + sh, sh))
                lo, sh = lo + sh, sh * 2
            cur, curid = pe_, pid
            bufs = [(sa3, ("Sa",)), (sb3, ("Sb",))]
            for si, (nlo, shf) in enumerate(steps):
                lastst = (si == len(steps) - 1)
                if lastst:
                    P.op("pool", lambda e, cur=cur, fin3=fin3, shf=shf, L=L: e.tensor_tensor(
                        out=fin3, in0=cur[:, :, Hp:L], in1=cur[:, :, Hp - shf:L - shf], op=ALU.add),
                        reads=[curid], writes=[fid])
                else:
                    dst, dstid = bufs[si % 2]
                    P.op("pool", lambda e, cur=cur, dst=dst, nlo=nlo, shf=shf, L=L: e.tensor_tensor(
                        out=dst[:, :, nlo:L], in0=cur[:, :, nlo:L], in1=cur[:, :, nlo - shf:L - shf], op=ALU.add),
                        reads=[curid], writes=[dstid])
                    cur, curid = dst, dstid
            pool_ctx[(g, gi)] = (fin3, fid, pe_, pid)

    def mix_pool_fin(g):
        G = mix_geom(g)
        sample, N, S, T, Hp = G["sample"], G["N"], G["S"], G["T"], G["Hp"]

        def v3(ap):
            return ap.rearrange("p (s t) -> p s t", s=S)
        for gi in range(4):
            w = WINDOWS[gi]
            fin3, fid, pe_, pid = pool_ctx.pop((g, gi))
            P.op("dve", lambda e, fin3=fin3, pe_=pe_, w=w, gi=gi: e.scalar_tensor_tensor(
                out=v3(dT[:, gi, 0:N]), in0=fin3, scalar=1.0 / w, in1=pe_[:, :, Hp:Hp + T],
                op0=ALU.mult, op1=ALU.subtract),
                reads=[fid, pid], writes=[("dT", gi)])
            if (not sample) and g == 0 and w > 1:
                P.op("dve", lambda e, fin3=fin3, w=w: e.tensor_tensor(
                    out=fin3[:, 0, 0:w - 1], in0=fin3[:, 0, 0:w - 1], in1=rc[:, 0:w - 1], op=ALU.mult),
                    reads=[fid, ("rc",)], writes=[fid])
                P.op("dve", lambda e, fin3=fin3, pe_=pe_, w=w, gi=gi: e.tensor_tensor(
                    out=dT[:, gi, 0:w - 1], in0=fin3[:, 0, 0:w - 1], in1=pe_[:, 0, Hp:Hp + w - 1],
                    op=ALU.subtract),
                    reads=[fid, pid, ("dT", gi)], writes=[("dT", gi)])
            if not sample:
                P.op("act", lambda e, pe_=pe_: e.activation(out=pe_[:, :, 0:16], in_=pe_[:, :, T:T + 16],
                                                            func=AF.Copy),
                     reads=[pid], writes=[pid])

    def mix_conv(g, i):
        G = mix_geom(g)
        sample, N, S, T = G["sample"], G["N"], G["S"], G["T"]
        mb = g % 2
        Hc = 2

        def v3(ap):
            return ap.rearrange("p (s t) -> p s t", s=S)
        cc = cv_cnt[0] // 3
        Bb, Bid = (Bsb, ("Bsb",)) if cc % 2 == 0 else (Bsb2[:, :], ("Bsb2",))
        tb, tid = (tbuf, ("tbuf",)) if cc % 2 == 0 else (tbuf2[:, :], ("tbuf2",))
        if sample:
            ve = vexts[:, i, :].rearrange("p (s e) -> p s e", s=SSEQ)
            vid = ("vexts", i)
        else:
            ve = vext[:, cc % 2, :].rearrange("p (s e) -> p s e", s=1)
            vid = ("vext", cc % 2)
            hid = ("vhalo", i)
            if g == 0:
                P.op("dve", lambda e, i=i: e.memset(vhalo[:, i, :], 0.0), writes=[hid])
            P.op("act", lambda e, ve=ve, i=i: e.activation(out=ve[:, 0, 0:2], in_=vhalo[:, i, :], func=AF.Copy),
                 reads=[hid], writes=[vid])
        hA = cv_cnt[0] % 4
        hB = (cv_cnt[0] + 1) % 4
        hC = (cv_cnt[0] + 2) % 4
        cv_cnt[0] += 3
        win_mm(G, "C", i, hA)
        win_mm(G, "u", i, hB)
        win_mm(G, "B", i, hC)
        P.op("act", lambda e, hA=hA: e.activation(out=Csb[:, 0:N], in_=psh(hA)[:, 0:N], func=AF.Copy),
             reads=[("ps", hA)], writes=[("Csb",)])
        P.op("dve", lambda e, ve=ve, hB=hB: e.tensor_tensor(out=ve[:, :, Hc:Hc + T], in0=v3(Csb[:, 0:N]),
                                                            in1=v3(psh(hB)[:, 0:N]), op=ALU.mult),
             reads=[("Csb",), ("ps", hB), vid], writes=[vid])
        P.op("act", lambda e, hC=hC, Bb=Bb: e.activation(out=Bb[:, 0:N], in_=psh(hC)[:, 0:N], func=AF.Copy),
             reads=[("ps", hC)], writes=[Bid])
        P.op("act", lambda e, ve=ve, i=i, tb=tb: e.activation(out=v3(tb[:, 0:N]), in_=ve[:, :, 2:2 + T], func=AF.Copy,
                                                              scale=cw[:, i * 3 + 2:i * 3 + 3]),
             reads=[vid, ("cw", i * 3 + 2)], writes=[tid])
        for k in (1, 0):
            P.op("dve", lambda e, ve=ve, i=i, k=k, tb=tb: e.scalar_tensor_tensor(
                out=v3(tb[:, 0:N]), in0=ve[:, :, k:k + T], scalar=cw[:, i * 3 + k:i * 3 + k + 1],
                in1=v3(tb[:, 0:N]), op0=ALU.mult, op1=ALU.add),
                reads=[vid, ("cw", i * 3 + k), tid], writes=[tid])
        P.op("dve", lambda e, i=i, tb=tb, Bb=Bb: e.tensor_tensor(out=mixT[:, mb, i, 0:N], in0=tb[:, 0:N],
                                                                 in1=Bb[:, 0:N], op=ALU.mult),
             reads=[tid, Bid], writes=[("mixT", mb, i)])
        if not sample:
            P.op("act", lambda e, ve=ve, i=i: e.activation(out=vhalo[:, i, :], in_=ve[:, 0, T:T + 2], func=AF.Copy),
                 reads=[vid], writes=[hid])

    def mix_pool_mm(g):
        G = mix_geom(g)
        N = G["N"]
        mb = g % 2
        PB = (4, 7)
        for pr in range(2):
            bank = PB[pr]

            def pe_fn(e, pr=pr, bank=bank):
                ins = None
                for gi in (2 * pr, 2 * pr + 1):
                    ins = e.matmul(ps[bank][:, (gi % 2) * 256:(gi % 2) * 256 + N], lhsT=poolw[:, gi, :],
                                   rhs=dT[:, gi, 0:N], start=True, stop=True)
                return ins
            P.op("pe", pe_fn, reads=[("poolw",), ("dT", 2 * pr), ("dT", 2 * pr + 1)], writes=[("ps", bank)])
            for gi in (2 * pr, 2 * pr + 1):
                P.op("act", lambda e, gi=gi, bank=bank: e.activation(
                    out=mixT[:, mb, 4 + gi, 0:N], in_=ps[bank][:, (gi % 2) * 256:(gi % 2) * 256 + N],
                    func=AF.Copy, scale=pscale[:, gi:gi + 1]),
                    reads=[("ps", bank), ("pscale", gi)], writes=[("mixT", mb, 4 + gi)])

    def mix_wout(g):
        G = mix_geom(g)
        mb = g % 2
        for ti, tt in enumerate(G["tiles"]):
            for dh in range(2):
                ob = 5 + ob_cnt[0] % 2
                ob_cnt[0] += 1

                def pe_fn(e, ti=ti, dh=dh, ob=ob):
                    ins = None
                    for c in range(8):
                        slot = mix_slots["o%d" % (c // 4)]
                        ins = e.matmul(ps[ob][:, 0:512], lhsT=mixT[:, mb, c, ti * 128:(ti + 1) * 128],
                                       rhs=ring_dn(slot)[:, c % 4, dh * 512:(dh + 1) * 512],
                                       start=(c == 0), stop=(c == 7))
                    return ins
                P.op("pe", pe_fn, reads=[("ring", mix_slots["o0"]), ("ring", mix_slots["o1"])] +
                     [("mixT", mb, c) for c in range(8)], writes=[("ps", ob)])
                xs_ = xres[:, tt, dh * 512:(dh + 1) * 512]
                P.op("dve", lambda e, xs_=xs_, ob=ob: e.tensor_tensor(out=xs_, in0=ps[ob][:, 0:512], in1=xs_,
                                                                      op=ALU.add),
                     reads=[("ps", ob), ("x", tt, dh)], writes=[("x", tt, dh)])

    def sample_state_prep_conv():
        def pe_fn(e):
            ins = None
            for i in range(4):
                ins = e.transpose(out=ps[5][:, i * 32:(i + 1) * 32], in_=sc_in[0:32, i * 128:(i + 1) * 128],
                                  identity=idf[0:32, 0:32])
            return ins
        P.op("pe", pe_fn, reads=[("sc_in",), ("idf",)], writes=[("ps", 5)])
        for i in range(4):
            ve = vexts[:, i, :].rearrange("p (s e) -> p s e", s=SSEQ)
            P.op("act", lambda e, ve=ve, i=i: e.activation(
                out=ve[:, :, 0:2], in_=ps[5][:, i * 32:(i + 1) * 32].rearrange("p (s e) -> p s e", s=SSEQ),
                func=AF.Copy),
                reads=[("ps", 5)], writes=[("vexts", i)])

    def sample_state_prep_pool_seq():
        seq = []
        for gi in range(4):
            for h in range(2):
                def emit(gi=gi, h=h):
                    bank = 7 if (gi * 2 + h) % 2 == 0 else 6
                    P.op("pe", lambda e: e.transpose(out=ps[bank][:, 0:120],
                                                     in_=sp_in[0:120, h, gi * 128:(gi + 1) * 128],
                                                     identity=idf[0:120, 0:120]),
                         reads=[("sp_in",), ("idf",)], writes=[("ps", bank)])
                    pe_ = pexts[:, gi, :].rearrange("p (s e) -> p s e", s=SSEQ)
                    P.op("act", lambda e: e.activation(
                        out=pe_[:, h * 8:(h + 1) * 8, 0:15],
                        in_=ps[bank][:, 0:120].rearrange("p (s e) -> p s e", s=8), func=AF.Copy),
                        reads=[("ps", bank)], writes=[("pexts", gi)])
                seq.append(emit)
        return seq

    out_dmas = []

    def prompt_state_out():
        P.alias([("sp_ncp",), ("sp_npp",)], [("sp_in",)])
        for i in range(4):
            P.op("pe", lambda e, i=i: e.transpose(out=ps[0][0:2, i * 128:(i + 1) * 128], in_=vhalo[:, i, :],
                                                  identity=idf[:, :]),
                 reads=[("vhalo", i), ("idf",)], writes=[("ps", 0)])
        P.op("act", lambda e: e.activation(out=sp_in[0:2, 1, :], in_=ps[0][0:2, 0:512], func=AF.Copy),
             reads=[("ps", 0)], writes=[("sp_ncp",)])
        out_dmas.append(P.op("sp", lambda e: e.dma_start(out=ncp[:, :], in_=sp_in[0:2, 1, :]),
                             reads=[("sp_ncp",)], dma=True))
        for gi in range(4):
            P.op("pe", lambda e, gi=gi: e.transpose(out=ps[1][0:15, gi * 128:(gi + 1) * 128], in_=pext[:, gi, 1:16],
                                                    identity=idf[:, :]),
                 reads=[("pext", gi), ("idf",)], writes=[("ps", 1)])
        P.op("act", lambda e: e.activation(out=sp_in[0:15, 0, :], in_=ps[1][0:15, 0:512], func=AF.Copy),
             reads=[("ps", 1)], writes=[("sp_npp",)])
        out_dmas.append(P.op("sp", lambda e: e.dma_start(out=npp[:, :], in_=sp_in[0:15, 0, :]),
                             reads=[("sp_npp",)], dma=True))
        P.alias([("sp_in",)], [("sp_ncp",), ("sp_npp",)])

    def stq_buf(h, gi):
        if h == 0:
            return vext[:, gi // 2, (gi % 2) * 128:(gi % 2) * 128 + 120]
        base = Bsb2 if gi < 2 else tbuf2
        return base[:, (gi % 2) * 128:(gi % 2) * 128 + 120]
    STQ_IDS = [("stq", h, gi) for h in range(2) for gi in range(4)]

    def sample_state_out_seq():
        stage, fin_ = [], []

        def conv_stage():
            for i in range(4):
                ve = vexts[:, i, :].rearrange("p (s e) -> p s e", s=SSEQ)
                P.op("act", lambda e, ve=ve, i=i: e.activation(
                    out=stgc[:, i, :].rearrange("p (s e) -> p s e", s=SSEQ), in_=ve[:, :, 8:10], func=AF.Copy),
                    reads=[("vexts", i)], writes=[("stgc", i)])

        def conv_fin():
            for i in range(4):
                P.op("pe", lambda e, i=i: e.transpose(out=ps[7][0:32, i * 128:(i + 1) * 128], in_=stgc[:, i, :],
                                                      identity=idf[:, :]),
                     reads=[("stgc", i), ("idf",)], writes=[("ps", 7)])
            P.op("act", lambda e: e.activation(out=sp_in[0:32, 0, :], in_=ps[7][0:32, 0:512], func=AF.Copy),
                 reads=[("ps", 7)], writes=[("sp_in",)])
            out_dmas.append(P.op("sp", lambda e: e.dma_start(out=ncs[:, :], in_=sp_in[0:32, 0, :]),
                                 reads=[("sp_in",)], dma=True))
        stage.append(conv_stage)
        fin_.append(conv_fin)
        for h in range(2):
            def pool_stage(h=h):
                for gi in range(4):
                    pe_ = pexts[:, gi, :].rearrange("p (s e) -> p s e", s=SSEQ)
                    P.op("act", lambda e, pe_=pe_, gi=gi: e.activation(
                        out=stq_buf(h, gi).rearrange("p (s e) -> p s e", s=8),
                        in_=pe_[:, h * 8:(h + 1) * 8, 8:23], func=AF.Copy),
                        reads=[("pexts", gi)], writes=[("stq", h, gi)])

            def pool_fin(h=h):
                bank = 6 if h == 0 else 7
                for gi in range(4):
                    P.op("pe", lambda e, gi=gi: e.transpose(
                        out=ps[bank][0:120, gi * 128:(gi + 1) * 128], in_=stq_buf(h, gi), identity=idf[:, :]),
                        reads=[("stq", h, gi), ("idf",)], writes=[("ps", bank)])
                P.op("act", lambda e: e.activation(out=sp_in[0:120, h, :], in_=ps[bank][0:120, 0:512],
                                                   func=AF.Copy),
                     reads=[("ps", bank)], writes=[("sp_in",)])
                if h == 1:
                    out_dmas.append(P.op("sp", lambda e: e.dma_start(out=nps.rearrange("(h r) c -> r h c", r=120),
                                                                     in_=sp_in[:, :, :]),
                                         reads=[("sp_in",)], dma=True))
            stage.append(pool_stage)
            fin_.append(pool_fin)
        return stage, fin_

    fin_cnt = [0]

    final_alt = [False]

    def final_last(tt):
        k = hp_of[tt]
        c = fin_cnt[0] % 2
        fin_cnt[0] += 1
        o_ap = ost[:, :] if c == 0 else ost2
        oid = ("ost", c)
        if final_alt[0]:
            P.op("act", lambda e: e.activation(out=o_ap, in_=xres[:, tt, :], func=AF.Copy, scale=stat(k, 2)),
                 reads=[("x", tt, 0), ("x", tt, 1), ("st", k)], writes=[oid])
            P.op("pool", lambda e: e.tensor_tensor(out=o_ap, in0=o_ap, in1=gbc[:, :], op=ALU.mult),
                 reads=[oid, ("gbc",)], writes=[oid])
            dst = yp[tt * 128:(tt + 1) * 128, :] if tt < 16 else ys[:, :]
            out_dmas.append(P.op("sp", lambda e: e.dma_start(out=dst, in_=o_ap), reads=[oid], dma=True))
            return
        P.op("dve", lambda e: e.scalar_tensor_tensor(out=o_ap, in0=xres[:, tt, :], scalar=stat(k, 2),
                                                     in1=gbc[:, :], op0=ALU.mult, op1=ALU.mult),
             reads=[("x", tt, 0), ("x", tt, 1), ("st", k), ("gbc",)], writes=[oid])
        dst = yp[tt * 128:(tt + 1) * 128, :] if tt < 16 else ys[:, :]
        out_dmas.append(P.op("sp", lambda e: e.dma_start(out=dst, in_=o_ap), reads=[oid], dma=True))

    def final_seq(tiles):
        seq = []
        for t in tiles:
            seq.append(lambda t=t: nA1(t))
            seq.append(lambda t=t: nA2(t))
            seq.append(lambda t=t: final_last(t))
        return seq

    def run_ffn(f, pre_hooks, post_fn, block_start_fn, tail_fn=None, post_now=False, defer_tail=False, lead=0,
                last_order=None, after_lead=None):
        items = [(b, g) for b in range(len(BLOCKS)) for g in range(len(FG))]
        if last_order is not None:
            items = items[:-len(FG)] + [(len(BLOCKS) - 1, g) for g in last_order]
        started = set()
        held = []
        prev = None
        pending = []
        for idx, (b, g) in enumerate(items):
            buf = idx % 2
            gu = ffn_GU_ops(f, b, g, buf)
            dd = ffn_D_ops(f, prev[0], prev[1], prev[2]) if prev is not None else []
            extra = list(pre_hooks.get((b, g), [])) + pending
            pending = []
            if idx == 0 and lead:
                for c in extra[:lead]:
                    c()
                extra = extra[lead:]
                if after_lead is not None:
                    after_lead()
            d_nb[0] = 2 if extra else 3
            dd = ffn_D_ops(f, prev[0], prev[1], prev[2]) if prev is not None else []
            if idx == 0 and f == 2:
                interleave(gu[:-1], merge(extra, dd))
                gu[-1]()
            else:
                interleave(gu, merge(extra, dd))
            d_nb[0] = 3
            if b not in started:
                started.add(b)
                block_start_fn(b)
            if prev is not None and prev[0] == len(BLOCKS) - 1:
                if post_now:
                    cl = post_fn(prev[1])
                    if idx == len(items) - 1:
                        for i_ in range(len(cl) // 3):
                            cl[3 * i_]()
                            cl[3 * i_ + 1]()
                        held = [cl[3 * i_ + 2] for i_ in range(len(cl) // 3)]
                    else:
                        for c in cl:
                            c()
                else:
                    pending = post_fn(prev[1])
            prev = (b, g, buf)
        if tail_fn is not None:
            tail_fn()
        if defer_tail:
            for emit in ffn_D_ops(f, prev[0], prev[1], prev[2]):
                emit()
            return pending + post_fn(prev[1])
        if post_now:
            dl = ffn_D_ops(f, prev[0], prev[1], prev[2])
            fl = post_fn(prev[1])
            nt_ = len(FG[prev[1]][2])
            assert len(dl) == 2 * nt_ and len(fl) == 3 * nt_ and not pending
            for ti in range(nt_):
                dl[2 * ti]()
                dl[2 * ti + 1]()
                if held:
                    final_alt[0] = True
                    held.pop(0)()
                    final_alt[0] = False
                if ti > 0:
                    fl[3 * (ti - 1) + 2]()
                fl[3 * ti]()
                fl[3 * ti + 1]()
            for c in held:
                c()
            fl[3 * (nt_ - 1) + 2]()
            return []
        for emit in merge(ffn_D_ops(f, prev[0], prev[1], prev[2]), pending):
            emit()
        for emit in post_fn(prev[1]):
            emit()
        return []

    norm_tiles(FG[0][2])
    load_first_block(2, len(BLOCKS[0]), False)

    def after_lead1():
        load_first_block(len(BLOCKS[0]), len(BLOCKS[0]), True)
        load_x_rest()
    pre1 = {}
    for g in range(1, len(FG)):
        pre1[(0, g - 1)] = norm_seq(FG[g][2])
    pre1[(2, 1)] = sample_state_prep_pool_seq()

    state = {}

    def ffn1_block_start(b):
        if b == 0:
            load_after[0] = (P.byeng["pe"][-1],)
            load_ffn_block(1, 1)
            load_after[0] = ()
            return
        if b + 1 < len(BLOCKS):
            load_ffn_block(1, b + 1)
        else:
            load_mix_units(["C", "u", "B"])

    def ffn1_post(g):
        if not state.get("gbc_mix"):
            load_gbc(norm_mix)
            state["gbc_mix"] = True
        return norm_seq(FG[g][2])

    late_norm = run_ffn(1, pre1, ffn1_post, ffn1_block_start, tail_fn=lambda: load_mix_units(["p", "o0"]),
                        defer_tail=True, lead=12, after_lead=after_lead1)

    def ffn2_slots(b):
        return (3, 0, 1) if b % 2 == 0 else (2, 4, 5)

    load_mix_units(["o1"])
    P.alias(MIX_SCR_IDS, FFN_SCR_IDS)
    tb_banks[0] = [7]

    def mixer_phase2_setup():
        nhp[0] = 3
        P.alias([("sc_in",)], [("hp", 3)])
        P.op("sp", lambda e: e.dma_start(out=sc_in[:, :], in_=sc_d[:, :]), writes=[("sc_in",)], dma=True)
        load_gbc(norm_ffn2)

    LT = FG[3][2] + FG[4][2]
    assert len(late_norm) == 4 * len(LT) and len(LT) == 5
    LATE_SCHED = {
        0: [(nA1, LT[0]), (nA2, LT[0]), (nA1, LT[1]), (nA2, LT[1])],
        1: [(nA1, LT[2]), (nA2, LT[2]), (nA3, LT[0])],
        2: [(nA1, LT[3]), (nA2, LT[3]), (nA3, LT[1]), (norm_B, LT[0])],
        3: [(nA3, LT[2]), (norm_B, LT[1])],
        4: [(nA1, LT[4]), (nA2, LT[4]), (nA3, LT[3]), (norm_B, LT[2])],
        5: [(nA3, LT[4]), (norm_B, LT[3])],
        6: [(norm_B, LT[4])],
    }
    trickle = {}
    for t in range(4):
        trickle.setdefault((t + 2, 1), []).append(lambda t=t: nA1(t))
        trickle.setdefault((t + 2, 2), []).append(lambda t=t: nA2(t))
        trickle.setdefault((t + 3, 1), []).append(lambda t=t: nA3(t))
        trickle.setdefault((t + 4, 1), []).append(lambda t=t: norm_B(t))
    tb_banks[0] = [5, 6]
    for g in range(9):
        if g == 2:
            tb_banks[0] = [7]
            mixer_phase2_setup()
        if g >= 2:
            mix_pool_mm(g - 1)
        if g == 4:
            sample_state_prep_conv()
        if g == 8:
            mix_conv(8, 0)
            mix_pool_win(8)
            load_ffn_block(2, 0, parts="g", slots=ffn2_slots(0))
            for i in range(1, 4):
                mix_conv(8, i)
            load_ffn_block(2, 0, parts="u", slots=ffn2_slots(0))
            P.alias([("hp", 3)], [("sc_in",)])
            nhp[0] = 4
            for t_ in FG[1][2]:
                nA1(t_)
                nA2(t_)
            load_ffn_block(2, 0, parts="d", slots=ffn2_slots(0))
            mix_wout(7)
            mix_pool_fin(8)
            prompt_state_out()
            mix_pool_mm(8)
            for t_ in FG[1][2]:
                nA3(t_)
            mix_wout(8)
            continue
        for i in range(4):
            if g < 2:
                for fn_, t_ in LATE_SCHED.get(g * 4 + i, []):
                    fn_(t_)
            mix_conv(g, i)
            if g == 0:
                if i == 2:
                    mix_pool_win(0)
            elif g == 1:
                if i == 0:
                    mix_pool_fin(0)
                if i == 1:
                    mix_pool_mm(0)
                if i == 2:
                    mix_pool_win(1)
                    mix_wout(0)
                    mix_pool_fin(1)
            else:
                if i == 0:
                    mix_pool_win(g)
                if i == 1:
                    mix_wout(g - 1)
                    mix_pool_fin(g)
            for c in trickle.get((g, i), []):
                c()
    P.alias(FFN_SCR_IDS + [("ost", 1)], MIX_SCR_IDS)
    P.alias([("ost", 0)], [("F", gi_) for gi_ in range(4)])
    P.alias(STQ_IDS, [("vext", 0), ("vext", 1), ("Bsb2",), ("tbuf2",)])
    tb_banks[0] = [7, 6]
    load_ffn_block(2, 1, parts="ud", slots=ffn2_slots(1))
    load_ffn_block(2, 1, parts="g", slots=ffn2_slots(1))

    def ffn2_block_start(b):
        if 1 <= b and b + 1 < len(BLOCKS):
            load_ffn_block(2, b + 1, slots=ffn2_slots(b + 1))

    def ffn2_post(g):
        if not state.get("gbc_fin"):
            load_gbc(norm_final)
            state["gbc_fin"] = True
        return final_seq(FG[g][2])

    pre2 = {}
    for g in range(1, len(FG)):
        pre2[(0, g - 1)] = norm_seq(FG[g][2])
    so_stage, so_fin = sample_state_out_seq()
    pre2[(0, 0)] = so_stage + [lambda t=t, i=i: norm_B(t, bank=4 + i % 2) for i, t in enumerate(FG[1][2])] + so_fin
    run_ffn(2, pre2, ffn2_post, ffn2_block_start, post_now=True, last_order=[4, 0, 1, 2, 3])

    fin = P.op("sp", lambda e: None)
    fin.deps = set(out_dmas)

    from contextlib import ExitStack
    with ExitStack() as es:
        sems = {e: es.enter_context(nc.semaphore("sem_" + e)) for e in Prog.ENGS}
        dma_sems = {
            "pool": [es.enter_context(nc.semaphore("dq_pool%d" % i)) for i in range(8)],
            "sp": [es.enter_context(nc.semaphore("dq_sp%d" % i)) for i in range(24)],
            "act": [], "pe": [], "dve": [],
        }
        P.finalize(nc, sems, dma_sems)
        block = es.enter_context(nc.Block())

        @block.sync
        def _(e):
            P.emit_engine("sp", e)

        @block.gpsimd
        def _(e):
            P.emit_engine("pool", e)

        @block.scalar
        def _(e):
            P.emit_engine("act", e)

        @block.vector
        def _(e):
            P.emit_engine("dve", e)

        @block.tensor
        def _(e):
            P.emit_engine("pe", e)
    return nc


_NC_CACHE = {}


def kernel(x_prompt, x_sample, state_conv, state_pool,
           norm_ffn1, ffn1_gate, ffn1_up, ffn1_down,
           norm_mix, w_in, conv_w, pool_w, pool_scale, w_out,
           norm_ffn2, ffn2_gate, ffn2_up, ffn2_down, norm_final):
    f32 = lambda a: np.ascontiguousarray(np.asarray(a, dtype=np.float32))
    x_prompt, x_sample, state_conv, state_pool = map(f32, (x_prompt, x_sample, state_conv, state_pool))
    shared = dict(norm_ffn1=f32(norm_ffn1), ffn1_gate=f32(ffn1_gate), ffn1_up=f32(ffn1_up), ffn1_down=f32(ffn1_down),
                  norm_mix=f32(norm_mix), w_in=f32(w_in), conv_w=f32(conv_w), pool_w=f32(pool_w),
                  pool_scale=f32(pool_scale), w_out=f32(w_out), norm_ffn2=f32(norm_ffn2),
                  ffn2_gate=f32(ffn2_gate), ffn2_up=f32(ffn2_up), ffn2_down=f32(ffn2_down),
                  norm_final=f32(norm_final))
    in_maps = []
    for c in range(NCORES):
        m = dict(shared)
        m["xp"] = x_prompt[c]
        m["xs"] = x_sample[c * SSEQ:(c + 1) * SSEQ].reshape(TS, D)
        m["sc"] = state_conv[c * SSEQ:(c + 1) * SSEQ].reshape(SSEQ * 2, CONVD)
        m["spl"] = state_pool[c * SSEQ:(c + 1) * SSEQ].reshape(SSEQ * 15, POOLD)
        in_maps.append(m)
    if "nc" not in _NC_CACHE:
        _NC_CACHE["nc"] = build_nc()
    nc = _NC_CACHE["nc"]
    res = run_bass_kernel_spmd(nc, in_maps, core_ids=list(range(NCORES)))
    r = res.results
    y_prompt = np.stack([r[c]["yp"] for c in range(NCORES)], axis=0)
    y_sample = np.concatenate([r[c]["ys"].reshape(SSEQ, ST, D) for c in range(NCORES)], axis=0)
    ncp = np.stack([r[c]["ncp"] for c in range(NCORES)], axis=0)
    npp = np.stack([r[c]["npp"] for c in range(NCORES)], axis=0)
    ncs = np.concatenate([r[c]["ncs"].reshape(SSEQ, 2, CONVD) for c in range(NCORES)], axis=0)
    nps = np.concatenate([r[c]["nps"].reshape(SSEQ, 15, POOLD) for c in range(NCORES)], axis=0)
    return (y_prompt.astype(np.float32), y_sample.astype(np.float32), ncp.astype(np.float32),
            npp.astype(np.float32), ncs.astype(np.float32), nps.astype(np.float32))
```

```python
import numpy as np
import concourse.bass as bass
import concourse.mybir as mybir
from concourse.bass_utils import run_bass_kernel_spmd

F32 = mybir.dt.float32
BF16 = mybir.dt.bfloat16
ALU = mybir.AluOpType
AF = mybir.ActivationFunctionType

NCORES = 8
D = 1024
DFF = 2816
KC = 8
TP = 2048
TS = 128
NT = 17
NTOK = TP + TS
SSEQ = 16
ST = 8
CONVD = 512
POOLD = 512
EPS = 1e-6
WINDOWS = (2, 4, 8, 16)
BLOCKS = [[0, 1, 2, 3], [4, 5, 6, 7], [8, 9, 10, 11], [12, 13, 14, 15], [16, 17, 18], [19, 20, 21]]
NRING = 6
SAME_ENG_SYNC = True


class Op:
    __slots__ = ("eng", "fn", "deps", "eidx", "sig", "sem", "val", "dma", "waits", "gidx", "extra_waits")


class Prog:
    ENGS = ("pe", "act", "dve", "pool", "sp")

    def __init__(self):
        self.order = []
        self.byeng = {e: [] for e in self.ENGS}
        self.res = {}

    def op(self, eng, fn, reads=(), writes=(), dma=False, after=()):
        o = Op()
        o.eng, o.fn, o.dma, o.sig = eng, fn, dma, False
        o.sem = None
        o.val = 0
        o.extra_waits = []
        deps = set()
        for r in reads:
            st = self.res.get(r)
            if st is not None and st[0] is not None:
                deps.add(st[0])
        for w in writes:
            st = self.res.get(w)
            if st is not None:
                if st[0] is not None:
                    deps.add(st[0])
                deps.update(st[1])
        for r in reads:
            st = self.res.setdefault(r, [None, []])
            st[1].append(o)
        for w in writes:
            self.res[w] = [o, []]
        deps.update(after)
        deps.discard(o)
        o.deps = deps
        o.gidx = len(self.order)
        o.eidx = len(self.byeng[eng])
        self.order.append(o)
        self.byeng[eng].append(o)
        return o

    def alias(self, new_ids, old_ids):
        acc = []
        for r in old_ids:
            st = self.res.get(r)
            if st is None:
                continue
            if st[0] is not None:
                acc.append(st[0])
            acc.extend(st[1])
        for n in new_ids:
            st = self.res.setdefault(n, [None, []])
            st[1].extend(acc)

    def finalize(self, nc, sems, dma_sems):
        for o in self.order:
            best = {}
            dmadeps = []
            for d in o.deps:
                if d.dma:
                    dmadeps.append(d)
                    continue
                if (not o.dma) and d.eng == o.eng and (o.eng == "pe" or not SAME_ENG_SYNC):
                    continue
                cur = best.get(d.eng)
                if cur is None or d.eidx > cur.eidx:
                    best[d.eng] = d
            o.waits = list(best.values()) + dmadeps
            for d in o.waits:
                d.sig = True
        cnt = {e: 0 for e in self.ENGS}
        dcnt = {e: 0 for e in self.ENGS}
        duse = {}
        for e in self.ENGS:
            for o in self.byeng[e]:
                if o.dma:
                    pool = dma_sems[e]
                    s = pool[dcnt[e] % len(pool)]
                    dcnt[e] += 1
                    prev = duse.get(s, 0)
                    if prev:
                        o.extra_waits.append((s, prev))
                    o.sem = s
                    o.val = prev + 16
                    duse[s] = o.val
                    o.sig = True
                elif o.sig:
                    cnt[e] += 1
                    o.sem = sems[e]
                    o.val = cnt[e]

    def emit_engine(self, e, eng):
        waited = {}
        for o in self.byeng[e]:
            ws = [(d.sem, d.val) for d in o.waits] + o.extra_waits
            for (s, v) in ws:
                if waited.get(s, 0) < v:
                    eng.wait_ge(s, v)
                    waited[s] = v
            ins = o.fn(eng)
            if o.sig and ins is not None:
                ins.then_inc(o.sem, 16 if o.dma else 1)


def build_nc():
    nc = bass.Bass("TRN2", target_bir_lowering=False)
    P = Prog()

    def din(name, shape):
        return nc.dram_tensor(name, list(shape), F32, kind="ExternalInput").ap()

    def dout(name, shape):
        return nc.dram_tensor(name, list(shape), F32, kind="ExternalOutput").ap()

    xp = din("xp", [TP, D])
    xs = din("xs", [TS, D])
    sc_d = din("sc", [SSEQ * 2, CONVD])
    spl_d = din("spl", [SSEQ * 15, POOLD])
    norm_ffn1 = din("norm_ffn1", [D])
    ffn1_gate = din("ffn1_gate", [D, DFF])
    ffn1_up = din("ffn1_up", [D, DFF])
    ffn1_down = din("ffn1_down", [DFF, D])
    norm_mix = din("norm_mix", [D])
    w_in = din("w_in", [D, 2048])
    conv_w = din("conv_w", [3, CONVD])
    pool_w = din("pool_w", [4, 128, 128])
    pool_scale = din("pool_scale", [POOLD])
    w_out = din("w_out", [D, D])
    norm_ffn2 = din("norm_ffn2", [D])
    ffn2_gate = din("ffn2_gate", [D, DFF])
    ffn2_up = din("ffn2_up", [D, DFF])
    ffn2_down = din("ffn2_down", [DFF, D])
    norm_final = din("norm_final", [D])

    yp = dout("yp", [TP, D])
    ys = dout("ys", [TS, D])
    ncp = dout("ncp", [2, CONVD])
    npp = dout("npp", [15, POOLD])
    ncs = dout("ncs", [SSEQ * 2, CONVD])
    nps = dout("nps", [SSEQ * 15, POOLD])

    xres = nc.alloc_sbuf_tensor("xres", [128, NT, D], F32)
    hT = nc.alloc_sbuf_tensor("hT", [128, KC, NTOK], BF16)
    ring = nc.alloc_sbuf_tensor("ring", [128, NRING, 4096], BF16)
    SCR = nc.alloc_sbuf_tensor("scr", [128, 4096], F32)
    hp = nc.alloc_sbuf_tensor("hp", [128, 3, D], BF16)
    gbc = nc.alloc_sbuf_tensor("gbc", [128, D], F32)
    ost = nc.alloc_sbuf_tensor("ost", [128, D], F32)
    idf = nc.alloc_sbuf_tensor("idf", [128, 128], F32)
    idb = nc.alloc_sbuf_tensor("idb", [128, 128], BF16)
    iot = nc.alloc_sbuf_tensor("iot", [128, 128], F32)
    stt = nc.alloc_sbuf_tensor("stt", [128, 16], F32)
    nhalf = nc.alloc_sbuf_tensor("nhalf", [128, 1], F32)
    deps_t = nc.alloc_sbuf_tensor("deps_t", [128, 1], F32)
    cw = nc.alloc_sbuf_tensor("cw", [128, 12], F32)
    pscale = nc.alloc_sbuf_tensor("pscale", [128, 4], F32)
    rc = nc.alloc_sbuf_tensor("rc", [128, 16], F32)
    poolw = nc.alloc_sbuf_tensor("poolw", [128, 4, 128], BF16)
    pext = nc.alloc_sbuf_tensor("pext", [128, 4, 16 + 256], F32)
    vexts = nc.alloc_sbuf_tensor("vexts", [128, 4, SSEQ * 10], F32)
    pexts = nc.alloc_sbuf_tensor("pexts", [128, 4, SSEQ * 23], F32)
    hp3 = nc.alloc_sbuf_tensor("hp3", [128, D], BF16)
    sc_in = hp3[0:32, :].bitcast(F32)
    sp_in = nc.alloc_sbuf_tensor("sp_in", [120, 2, POOLD], F32)
    stgc = nc.alloc_sbuf_tensor("stgc", [128, 4, 32], F32)
    stgp = nc.alloc_sbuf_tensor("stgp", [128, 240], F32)

    aT = SCR[:, 0:2048].bitcast(BF16).rearrange("p (b j t) -> p b j t", b=2, j=4)
    sg = SCR[:, 2048:3072].rearrange("p (k t) -> p k t", k=2)
    ost2 = SCR[:, 3072:4096]
    mixT = SCR[:, 0:2048].bitcast(BF16).rearrange("p (b c t) -> p b c t", b=2, c=8)
    Csb = SCR[:, 2048:2304]
    tbuf = SCR[:, 2304:2560]
    Sa = SCR[:, 2560:2928]
    Sb = SCR[:, 2928:3296]
    dT = SCR[:, 3296:3808].bitcast(BF16).rearrange("p (g t) -> p g t", g=4)
    Bsb = SCR[:, 3808:4064]
    vext = nc.alloc_sbuf_tensor("vext", [128, 2, 258], F32)
    vhalo = nc.alloc_sbuf_tensor("vhalo", [128, 4, 2], F32)
    Bsb2 = nc.alloc_sbuf_tensor("Bsb2", [128, 256], F32)
    tbuf2 = nc.alloc_sbuf_tensor("tbuf2", [128, 256], F32)
    FFN_SCR_IDS = [("aT", b, j) for b in range(2) for j in range(4)] + [("sg", 0), ("sg", 1)]
    MIX_SCR_IDS = [("mixT", b, c) for b in range(2) for c in range(8)] + [("Csb",), ("Bsb",), ("tbuf",), ("Sa",), ("Sb",)] + \
                  [("dT", gi) for gi in range(4)]

    ps = [nc.alloc_psum_tensor("ps%d" % b, [128, 512], F32) for b in range(8)]

    def ring_gu(s):
        return ring[:, s, :].rearrange("p (k f) -> p k f", k=8)

    def ring_dn(s):
        return ring[:, s, :].rearrange("p (j d) -> p j d", j=4)

    P.op("pool", lambda e: e.iota(iot[:, :], [[1, 128]], base=0, channel_multiplier=-1,
                                  allow_small_or_imprecise_dtypes=True), writes=[("iot",)])
    P.op("dve", lambda e: e.tensor_single_scalar(out=idf[:, :], in_=iot[:, :], scalar=0.0, op=ALU.is_equal),
         reads=[("iot",)], writes=[("idf",)])
    P.op("dve", lambda e: e.tensor_copy(out=idb[:, :], in_=idf[:, :]), reads=[("idf",)], writes=[("idb",)])
    P.op("pool", lambda e: e.memset(nhalf[:, :], -0.5), writes=[("nhalf",)])
    P.op("pool", lambda e: e.memset(deps_t[:, :], float(D) * EPS), writes=[("deps_t",)])
    P.op("pool", lambda e: e.iota(rc[:, :], [[1, 16]], base=1, channel_multiplier=0,
                                  allow_small_or_imprecise_dtypes=True), writes=[("rc",)])
    P.op("dve", lambda e: e.reciprocal(out=rc[:, :], in_=rc[:, :]), reads=[("rc",)], writes=[("rc",)])
    P.op("act", lambda e: e.activation(out=stt[:, 15:16], in_=nhalf[:, :], func=AF.Square),
         reads=[("nhalf",)], writes=[("warm",)])

    slot_next = [0]

    load_after = [()]
    unit_ops = {}

    def load_unit(kind, W, c0, nb, slot=None):
        if slot is None:
            s = slot_next[0] % NRING
            slot_next[0] += 1
        else:
            s = slot
        if kind == "gu":
            src = W.rearrange("(k p) f -> p k f", p=128)[:, :, c0 * 128:(c0 + nb) * 128]
            dst = ring_gu(s)[:, :, 0:nb * 128]
        else:
            src = W[c0 * 128:(c0 + nb) * 128, :].rearrange("(j p) d -> p j d", p=128)
            dst = ring_dn(s)[:, 0:nb, :]
        unit_ops[s] = P.op("pool", lambda e, dst=dst, src=src: e.dma_start(out=dst, in_=src),
                           writes=[("ring", s)], dma=True, after=load_after[0])
        return s

    ffn_w = {1: (ffn1_gate, ffn1_up, ffn1_down), 2: (ffn2_gate, ffn2_up, ffn2_down)}
    ffn_slots = {}

    ffn_part = {}

    def load_ffn_block(f, b, parts="gud", slots=None):
        blk = BLOCKS[b]
        Wg, Wu, Wd = ffn_w[f]
        d = ffn_part.setdefault((f, b), {})
        for i, pch in enumerate("gud"):
            if pch not in parts:
                continue
            W = (Wg, Wu, Wd)[i]
            sl = None if slots is None else slots[i]
            d[pch] = load_unit("gu" if pch != "d" else "dn", W, blk[0], len(blk), slot=sl)
        if len(d) == 3:
            ffn_slots[(f, b)] = (d["g"], d["u"], d["d"])

    mix_slots = {}

    def load_mix_units(names):
        for nm in names:
            if nm in ("C", "u", "B", "p"):
                q = {"B": 0, "C": 1, "u": 2, "p": 3}[nm]
                mix_slots[nm] = load_unit("gu", w_in, q * 4, 4)
            else:
                h = int(nm[1])
                mix_slots[nm] = load_unit("dn", w_out, h * 4, 4)

    def load_first_block(j0, j1, with_down):
        blk = BLOCKS[0]
        gs_, us_, ds_ = 0, 1, 2
        slot_next[0] = 3
        for jj in range(j0, j1):
            for W, s_ in ((ffn1_gate, gs_), (ffn1_up, us_)):
                src = W.rearrange("(k p) f -> p k f", p=128)[:, :, (blk[0] + jj) * 128:(blk[0] + jj + 1) * 128]
                dst = ring_gu(s_)[:, :, jj * 128:(jj + 1) * 128]
                unit_ops[s_] = P.op("pool", lambda e, dst=dst, src=src: e.dma_start(out=dst, in_=src),
                                    writes=[("ringc", s_, jj)], dma=True)
        if with_down:
            load_unit("dn", ffn1_down, blk[0], len(blk), slot=ds_)
        ffn_slots[(1, 0)] = (gs_, us_, ds_)
    load_first_block(0, 2, False)

    xload_ops = []
    xload_after = [()]

    def load_x(tt):
        src = xp[tt * 128:(tt + 1) * 128, :] if tt < 16 else xs[:, :]
        xload_ops.append(P.op("sp", lambda e: e.dma_start(out=xres[:, tt, :], in_=src),
                              writes=[("x", tt, 0), ("x", tt, 1)], dma=True, after=xload_after[0]))

    def load_gbc(g_ap):
        P.op("sp", lambda e: e.dma_start(out=gbc[:, :], in_=g_ap.partition_broadcast(128)),
             writes=[("gbc",)], dma=True)
        P.op("dve", lambda e: e.tensor_scalar(out=gbc[:, :], in0=gbc[:, :], scalar1=float(D) ** 0.5, scalar2=None,
                                              op0=ALU.mult),
             reads=[("gbc",)], writes=[("gbc",)])

    load_gbc(norm_ffn1)
    for tt in range(4):
        load_x(tt)
    gs0, us0, ds0 = ffn_slots[(1, 0)]
    xload_after[0] = (unit_ops[gs0], unit_ops[us0])
    for tt in range(4, 8):
        load_x(tt)
    xload_after[0] = ()

    def load_x_rest():
        xload_after[0] = (unit_ops[ds0],)
        for tt in range(8, NT):
            load_x(tt)
        xload_after[0] = ()
    for i in range(4):
        for k in range(3):
            src = conv_w[k, i * 128:(i + 1) * 128].rearrange("(p o) -> p o", o=1)
            P.op("sp", lambda e, i=i, k=k, src=src: e.dma_start(out=cw[:, i * 3 + k:i * 3 + k + 1], in_=src),
                 writes=[("cw", i * 3 + k)], dma=True)
        src = pool_scale[i * 128:(i + 1) * 128].rearrange("(p o) -> p o", o=1)
        P.op("sp", lambda e, i=i, src=src: e.dma_start(out=pscale[:, i:i + 1], in_=src),
             writes=[("pscale", i)], dma=True)
    P.op("sp", lambda e: e.dma_start(out=sp_in[:, :, :], in_=spl_d.rearrange("(h r) c -> r h c", r=120)),
         writes=[("sp_in",)], dma=True)
    P.op("pool", lambda e: e.dma_start(out=poolw[:, :, :], in_=pool_w.rearrange("g c d -> c g d")),
         writes=[("poolw",)], dma=True)

    hp_cnt = [0]
    hp_of = {}

    def stat(k, j):
        return stt[:, k * 4 + j:k * 4 + j + 1]

    nhp = [4]

    def hpbuf(k):
        return hp[:, k, :] if k < 3 else hp3[:, :]

    def nA1(tt):
        k = hp_cnt[0] % nhp[0]
        hp_cnt[0] += 1
        hp_of[tt] = k
        P.op("act", lambda e: e.activation(out=hpbuf(k), in_=xres[:, tt, :], func=AF.Square,
                                           accum_out=stat(k, 0)),
             reads=[("x", tt, 0), ("x", tt, 1)], writes=[("hp", k), ("st", k)])

    def nA2(tt):
        k = hp_of[tt]
        P.op("pool", lambda e: e.tensor_tensor(out=stat(k, 1), in0=stat(k, 0), in1=deps_t[:, :], op=ALU.add),
             reads=[("st", k), ("deps_t",)], writes=[("st", k)])
        P.op("pool", lambda e: e.tensor_tensor(out=stat(k, 2), in0=stat(k, 1), in1=nhalf[:, :], op=ALU.pow),
             reads=[("st", k), ("nhalf",)], writes=[("st", k)])

    def nA3(tt):
        k = hp_of[tt]
        P.op("dve", lambda e: e.scalar_tensor_tensor(out=hpbuf(k), in0=xres[:, tt, :], scalar=stat(k, 2),
                                                     in1=gbc[:, :], op0=ALU.mult, op1=ALU.mult),
             reads=[("x", tt, 0), ("x", tt, 1), ("st", k), ("gbc",)], writes=[("hp", k)])

    tb_cnt = [0]
    tb_banks = [[7, 6]]

    def norm_B(tt, bank=None):
        k = hp_of[tt]
        if bank is None:
            bank = tb_banks[0][tb_cnt[0] % len(tb_banks[0])]
            tb_cnt[0] += 1
        pst = ps[bank][:, :].bitcast(BF16)

        def pe_fn(e):
            ins = None
            for kc in range(KC):
                ins = e.transpose(out=pst[:, kc * 128:(kc + 1) * 128], in_=hpbuf(k)[:, kc * 128:(kc + 1) * 128],
                                  identity=idb[:, :])
            return ins
        P.op("pe", pe_fn, reads=[("hp", k), ("idb",)], writes=[("ps", bank)])
        P.op("act", lambda e: e.activation(out=hT[:, :, tt * 128:(tt + 1) * 128],
                                           in_=pst.rearrange("p (a b) -> p a b", a=KC), func=AF.Copy),
             reads=[("ps", bank)], writes=[("hT", tt)])

    def norm_seq(tiles, last=None):
        if last is None:
            last = norm_B
        n = len(tiles)
        NHP = nhp[0]
        m = min(NHP, n)
        seq = []
        for step in (nA1, nA2, nA3):
            for t in tiles[:m]:
                seq.append(lambda t=t, step=step: step(t))
        for i in range(n):
            seq.append(lambda t=tiles[i]: last(t))
            if i + NHP < n:
                for step in (nA1, nA2, nA3):
                    seq.append(lambda t=tiles[i + NHP], step=step: step(t))
        return seq

    def norm_tiles(tiles):
        for c in norm_seq(tiles):
            c()

    def merge(a, b):
        out = []
        na, nb_ = len(a), len(b)
        ia = ib = 0
        while ia < na or ib < nb_:
            if ib >= nb_ or (ia < na and ia * nb_ <= ib * na):
                out.append(a[ia])
                ia += 1
            else:
                out.append(b[ib])
                ib += 1
        return out

    FG = [(g * 512, 512, [4 * g + i for i in range(4)]) for g in range(4)] + [(TP, TS, [16])]

    gu_cnt = [0]
    d_cnt = [0]
    d_nb = [3]
    item_cnt = [0]

    def ffn_GU_ops(f, b, g, buf):
        blk = BLOCKS[b]
        tok0, N, tiles = FG[g]
        gs, us, _ = ffn_slots[(f, b)]
        outl = []
        for jj in range(len(blk)):
            def emit(jj=jj):
                c = gu_cnt[0]
                gu_cnt[0] += 1
                gb, ub, sk = c % 2, 2 + c % 2, c % 2
                hreads = [("hT", tt) for tt in tiles]
                if (f, b) == (1, 0):
                    hreads = hreads + [("ringc", gs, jj), ("ringc", us, jj)]

                def mk(slot, bank):
                    def pe_fn(e):
                        ins = None
                        for kc in range(KC):
                            ins = e.matmul(ps[bank][:, 0:N], lhsT=ring_gu(slot)[:, kc, jj * 128:(jj + 1) * 128],
                                           rhs=hT[:, kc, tok0:tok0 + N], start=(kc == 0), stop=(kc == KC - 1))
                        return ins
                    return pe_fn
                P.op("pe", mk(gs, gb), reads=[("ring", gs)] + hreads, writes=[("ps", gb)])
                P.op("pe", mk(us, ub), reads=[("ring", us)] + hreads, writes=[("ps", ub)])
                P.op("act", lambda e: e.activation(out=sg[:, sk, 0:N], in_=ps[gb][:, 0:N], func=AF.Silu),
                     reads=[("ps", gb)], writes=[("sg", sk)])
                P.op("dve", lambda e: e.tensor_tensor(out=aT[:, buf, jj, 0:N], in0=sg[:, sk, 0:N],
                                                      in1=ps[ub][:, 0:N], op=ALU.mult),
                     reads=[("sg", sk), ("ps", ub)], writes=[("aT", buf, jj)])
            outl.append(emit)
        return outl

    def ffn_D_ops(f, b, g, buf):
        blk = BLOCKS[b]
        nb = len(blk)
        tok0, N, tiles = FG[g]
        _, _, ds = ffn_slots[(f, b)]
        outl = []
        for ti, tt in enumerate(tiles):
            for dh in range(2):
                def emit(ti=ti, tt=tt, dh=dh):
                    c = d_cnt[0]
                    d_cnt[0] += 1
                    db = 4 + c % d_nb[0]

                    def pe_fn(e):
                        ins = None
                        for jj in range(nb):
                            ins = e.matmul(ps[db][:, 0:512], lhsT=aT[:, buf, jj, ti * 128:(ti + 1) * 128],
                                           rhs=ring_dn(ds)[:, jj, dh * 512:(dh + 1) * 512],
                                           start=(jj == 0), stop=(jj == nb - 1))
                        return ins
                    P.op("pe", pe_fn, reads=[("ring", ds)] + [("aT", buf, jj) for jj in range(nb)],
                         writes=[("ps", db)])
                    xs_ = xres[:, tt, dh * 512:(dh + 1) * 512]
                    P.op("dve", lambda e: e.scalar_tensor_tensor(out=xs_, in0=ps[db][:, 0:512], scalar=0.5,
                                                                 in1=xs_, op0=ALU.mult, op1=ALU.add),
                         reads=[("ps", db), ("x", tt, dh)], writes=[("x", tt, dh)])
                outl.append(emit)
        return outl

    def interleave(a, b):
        na, nb_ = len(a), len(b)
        ia = ib = 0
        while ia < na or ib < nb_:
            if ia < na:
                a[ia]()
                ia += 1
            while ib < nb_ and (ia >= na or ib * na < ia * nb_):
                b[ib]()
                ib += 1

    ob_cnt = [0]
    cv_cnt = [0]

    def psh(h):
        return ps[h][:, 0:256]

    def mix_geom(g):
        if g == 8:
            return dict(sample=True, tok0=TP, N=TS, tiles=[16], S=SSEQ, T=ST, Hp=15)
        return dict(sample=False, tok0=g * 256, N=256, tiles=[2 * g, 2 * g + 1], S=1, T=256, Hp=16)

    def win_mm(G, nm, col, h):
        slot = mix_slots[nm]
        N, tok0 = G["N"], G["tok0"]

        def pe_fn(e):
            ins = None
            for kc in range(KC):
                ins = e.matmul(psh(h)[:, 0:N], lhsT=ring_gu(slot)[:, kc, col * 128:(col + 1) * 128],
                               rhs=hT[:, kc, tok0:tok0 + N], start=(kc == 0), stop=(kc == KC - 1))
            return ins
        P.op("pe", pe_fn, reads=[("ring", slot)] + [("hT", tt) for tt in G["tiles"]], writes=[("ps", h)])

    pool_ctx = {}

    def mix_pool_win(g):
        G = mix_geom(g)
        sample, N, S, T, Hp = G["sample"], G["N"], G["S"], G["T"], G["Hp"]

        def v3(ap):
            return ap.rearrange("p (s t) -> p s t", s=S)
        PB = (4, 7)
        for pr in range(2):
            bank = PB[pr]
            slot = mix_slots["p"]

            def pe_fn(e, pr=pr, bank=bank, slot=slot):
                ins = None
                for gi in (2 * pr, 2 * pr + 1):
                    for kc in range(KC):
                        ins = e.matmul(ps[bank][:, (gi % 2) * 256:(gi % 2) * 256 + N],
                                       lhsT=ring_gu(slot)[:, kc, gi * 128:(gi + 1) * 128],
                                       rhs=hT[:, kc, G["tok0"]:G["tok0"] + N], start=(kc == 0), stop=(kc == KC - 1))
                return ins
            P.op("pe", pe_fn, reads=[("ring", slot)] + [("hT", tt) for tt in G["tiles"]], writes=[("ps", bank)])
        for gi in range(4):
            w = WINDOWS[gi]
            pbank = PB[gi // 2]
            pcol = (gi % 2) * 256
            if sample:
                pe_ = pexts[:, gi, :].rearrange("p (s e) -> p s e", s=SSEQ)
                pid = ("pexts", gi)
            else:
                pe_ = pext[:, gi, :].rearrange("p (s e) -> p s e", s=1)
                pid = ("pext", gi)
                if g == 0:
                    P.op("dve", lambda e, pe_=pe_: e.memset(pe_[:, :, 0:16], 0.0), writes=[pid])
            L = Hp + T
            P.op("act", lambda e, pe_=pe_, pbank=pbank, pcol=pcol: e.activation(
                out=pe_[:, :, Hp:Hp + T], in_=v3(ps[pbank][:, pcol:pcol + N]), func=AF.Copy),
                reads=[("ps", pbank), pid], writes=[pid])
            sa3 = Sa[:, 0:S * L].rearrange("p (s e) -> p s e", s=S)
            sb3 = Sb[:, 0:S * L].rearrange("p (s e) -> p s e", s=S)
            fin3 = ost[:, gi * 256:gi * 256 + N].rearrange("p (s t) -> p s t", s=S)
            fid = ("F", gi)
            steps = []
            lo, sh = 0, 1
            while sh < w:
                steps.append((lo + sh, sh))
                lo, sh = lo + sh, sh * 2
            cur, curid = pe_, pid
            bufs = [(sa3, ("Sa",)), (sb3, ("Sb",))]
            for si, (nlo, shf) in enumerate(steps):
                lastst = (si == len(steps) - 1)
                if lastst:
                    P.op("pool", lambda e, cur=cur, fin3=fin3, shf=shf, L=L: e.tensor_tensor(
                        out=fin3, in0=cur[:, :, Hp:L], in1=cur[:, :, Hp - shf:L - shf], op=ALU.add),
                        reads=[curid], writes=[fid])
                else:
                    dst, dstid = bufs[si % 2]
                    P.op("pool", lambda e, cur=cur, dst=dst, nlo=nlo, shf=shf, L=L: e.tensor_tensor(
                        out=dst[:, :, nlo:L], in0=cur[:, :, nlo:L], in1=cur[:, :, nlo - shf:L - shf], op=ALU.add),
                        reads=[curid], writes=[dstid])
                    cur, curid = dst, dstid
            pool_ctx[(g, gi)] = (fin3, fid, pe_, pid)

    def mix_pool_fin(g):
        G = mix_geom(g)
        sample, N, S, T, Hp = G["sample"], G["N"], G["S"], G["T"], G["Hp"]

        def v3(ap):
            return ap.rearrange("p (s t) -> p s t", s=S)
        for gi in range(4):
            w = WINDOWS[gi]
            fin3, fid, pe_, pid = pool_ctx.pop((g, gi))
            P.op("dve", lambda e, fin3=fin3, pe_=pe_, w=w, gi=gi: e.scalar_tensor_tensor(
                out=v3(dT[:, gi, 0:N]), in0=fin3, scalar=1.0 / w, in1=pe_[:, :, Hp:Hp + T],
                op0=ALU.mult, op1=ALU.subtract),
                reads=[fid, pid], writes=[("dT", gi)])
            if (not sample) and g == 0 and w > 1:
                P.op("dve", lambda e, fin3=fin3, w=w: e.tensor_tensor(
                    out=fin3[:, 0, 0:w - 1], in0=fin3[:, 0, 0:w - 1], in1=rc[:, 0:w - 1], op=ALU.mult),
                    reads=[fid, ("rc",)], writes=[fid])
                P.op("dve", lambda e, fin3=fin3, pe_=pe_, w=w, gi=gi: e.tensor_tensor(
                    out=dT[:, gi, 0:w - 1], in0=fin3[:, 0, 0:w - 1], in1=pe_[:, 0, Hp:Hp + w - 1],
                    op=ALU.subtract),
                    reads=[fid, pid, ("dT", gi)], writes=[("dT", gi)])
            if not sample:
                P.op("act", lambda e, pe_=pe_: e.activation(out=pe_[:, :, 0:16], in_=pe_[:, :, T:T + 16],
                                                            func=AF.Copy),
                     reads=[pid], writes=[pid])

    def mix_conv(g, i):
        G = mix_geom(g)
        sample, N, S, T = G["sample"], G["N"], G["S"], G["T"]
        mb = g % 2
        Hc = 2

        def v3(ap):
            return ap.rearrange("p (s t) -> p s t", s=S)
        cc = cv_cnt[0] // 3
        Bb, Bid = (Bsb, ("Bsb",)) if cc % 2 == 0 else (Bsb2[:, :], ("Bsb2",))
        tb, tid = (tbuf, ("tbuf",)) if cc % 2 == 0 else (tbuf2[:, :], ("tbuf2",))
        if sample:
            ve = vexts[:, i, :].rearrange("p (s e) -> p s e", s=SSEQ)
            vid = ("vexts", i)
        else:
            ve = vext[:, cc % 2, :].rearrange("p (s e) -> p s e", s=1)
            vid = ("vext", cc % 2)
            hid = ("vhalo", i)
            if g == 0:
                P.op("dve", lambda e, i=i: e.memset(vhalo[:, i, :], 0.0), writes=[hid])
            P.op("act", lambda e, ve=ve, i=i: e.activation(out=ve[:, 0, 0:2], in_=vhalo[:, i, :], func=AF.Copy),
                 reads=[hid], writes=[vid])
        hA = cv_cnt[0] % 4
        hB = (cv_cnt[0] + 1) % 4
        hC = (cv_cnt[0] + 2) % 4
        cv_cnt[0] += 3
        win_mm(G, "C", i, hA)
        win_mm(G, "u", i, hB)
        win_mm(G, "B", i, hC)
        P.op("act", lambda e, hA=hA: e.activation(out=Csb[:, 0:N], in_=psh(hA)[:, 0:N], func=AF.Copy),
             reads=[("ps", hA)], writes=[("Csb",)])
        P.op("dve", lambda e, ve=ve, hB=hB: e.tensor_tensor(out=ve[:, :, Hc:Hc + T], in0=v3(Csb[:, 0:N]),
                                                            in1=v3(psh(hB)[:, 0:N]), op=ALU.mult),
             reads=[("Csb",), ("ps", hB), vid], writes=[vid])
        P.op("act", lambda e, hC=hC, Bb=Bb: e.activation(out=Bb[:, 0:N], in_=psh(hC)[:, 0:N], func=AF.Copy),
             reads=[("ps", hC)], writes=[Bid])
        P.op("act", lambda e, ve=ve, i=i, tb=tb: e.activation(out=v3(tb[:, 0:N]), in_=ve[:, :, 2:2 + T], func=AF.Copy,
                                                              scale=cw[:, i * 3 + 2:i * 3 + 3]),
             reads=[vid, ("cw", i * 3 + 2)], writes=[tid])
        for k in (1, 0):
            P.op("dve", lambda e, ve=ve, i=i, k=k, tb=tb: e.scalar_tensor_tensor(
                out=v3(tb[:, 0:N]), in0=ve[:, :, k:k + T], scalar=cw[:, i * 3 + k:i * 3 + k + 1],
                in1=v3(tb[:, 0:N]), op0=ALU.mult, op1=ALU.add),
                reads=[vid, ("cw", i * 3 + k), tid], writes=[tid])
        P.op("dve", lambda e, i=i, tb=tb, Bb=Bb: e.tensor_tensor(out=mixT[:, mb, i, 0:N], in0=tb[:, 0:N],
                                                                 in1=Bb[:, 0:N], op=ALU.mult),
             reads=[tid, Bid], writes=[("mixT", mb, i)])
        if not sample:
            P.op("act", lambda e, ve=ve, i=i: e.activation(out=vhalo[:, i, :], in_=ve[:, 0, T:T + 2], func=AF.Copy),
                 reads=[vid], writes=[hid])

    def mix_pool_mm(g):
        G = mix_geom(g)
        N = G["N"]
        mb = g % 2
        PB = (4, 7)
        for pr in range(2):
            bank = PB[pr]

            def pe_fn(e, pr=pr, bank=bank):
                ins = None
                for gi in (2 * pr, 2 * pr + 1):
                    ins = e.matmul(ps[bank][:, (gi % 2) * 256:(gi % 2) * 256 + N], lhsT=poolw[:, gi, :],
                                   rhs=dT[:, gi, 0:N], start=True, stop=True)
                return ins
            P.op("pe", pe_fn, reads=[("poolw",), ("dT", 2 * pr), ("dT", 2 * pr + 1)], writes=[("ps", bank)])
            for gi in (2 * pr, 2 * pr + 1):
                P.op("act", lambda e, gi=gi, bank=bank: e.activation(
                    out=mixT[:, mb, 4 + gi, 0:N], in_=ps[bank][:, (gi % 2) * 256:(gi % 2) * 256 + N],
                    func=AF.Copy, scale=pscale[:, gi:gi + 1]),
                    reads=[("ps", bank), ("pscale", gi)], writes=[("mixT", mb, 4 + gi)])

    def mix_wout(g):
        G = mix_geom(g)
        mb = g % 2
        for ti, tt in enumerate(G["tiles"]):
            for dh in range(2):
                ob = 5 + ob_cnt[0] % 2
                ob_cnt[0] += 1

                def pe_fn(e, ti=ti, dh=dh, ob=ob):
                    ins = None
                    for c in range(8):
                        slot = mix_slots["o%d" % (c // 4)]
                        ins = e.matmul(ps[ob][:, 0:512], lhsT=mixT[:, mb, c, ti * 128:(ti + 1) * 128],
                                       rhs=ring_dn(slot)[:, c % 4, dh * 512:(dh + 1) * 512],
                                       start=(c == 0), stop=(c == 7))
                    return ins
                P.op("pe", pe_fn, reads=[("ring", mix_slots["o0"]), ("ring", mix_slots["o1"])] +
                     [("mixT", mb, c) for c in range(8)], writes=[("ps", ob)])
                xs_ = xres[:, tt, dh * 512:(dh + 1) * 512]
                P.op("dve", lambda e, xs_=xs_, ob=ob: e.tensor_tensor(out=xs_, in0=ps[ob][:, 0:512], in1=xs_,
                                                                      op=ALU.add),
                     reads=[("ps", ob), ("x", tt, dh)], writes=[("x", tt, dh)])

    def sample_state_prep_conv():
        def pe_fn(e):
            ins = None
            for i in range(4):
                ins = e.transpose(out=ps[5][:, i * 32:(i + 1) * 32], in_=sc_in[0:32, i * 128:(i + 1) * 128],
                                  identity=idf[0:32, 0:32])
            return ins
        P.op("pe", pe_fn, reads=[("sc_in",), ("idf",)], writes=[("ps", 5)])
        for i in range(4):
            ve = vexts[:, i, :].rearrange("p (s e) -> p s e", s=SSEQ)
            P.op("act", lambda e, ve=ve, i=i: e.activation(
                out=ve[:, :, 0:2], in_=ps[5][:, i * 32:(i + 1) * 32].rearrange("p (s e) -> p s e", s=SSEQ),
                func=AF.Copy),
                reads=[("ps", 5)], writes=[("vexts", i)])

    def sample_state_prep_pool_seq():
        seq = []
        for gi in range(4):
            for h in range(2):
                def emit(gi=gi, h=h):
                    bank = 7 if (gi * 2 + h) % 2 == 0 else 6
                    P.op("pe", lambda e: e.transpose(out=ps[bank][:, 0:120],
                                                     in_=sp_in[0:120, h, gi * 128:(gi + 1) * 128],
                                                     identity=idf[0:120, 0:120]),
                         reads=[("sp_in",), ("idf",)], writes=[("ps", bank)])
                    pe_ = pexts[:, gi, :].rearrange("p (s e) -> p s e", s=SSEQ)
                    P.op("act", lambda e: e.activation(
                        out=pe_[:, h * 8:(h + 1) * 8, 0:15],
                        in_=ps[bank][:, 0:120].rearrange("p (s e) -> p s e", s=8), func=AF.Copy),
                        reads=[("ps", bank)], writes=[("pexts", gi)])
                seq.append(emit)
        return seq

    out_dmas = []

    def prompt_state_out():
        P.alias([("sp_ncp",), ("sp_npp",)], [("sp_in",)])
        for i in range(4):
            P.op("pe", lambda e, i=i: e.transpose(out=ps[0][0:2, i * 128:(i + 1) * 128], in_=vhalo[:, i, :],
                                                  identity=idf[:, :]),
                 reads=[("vhalo", i), ("idf",)], writes=[("ps", 0)])
        P.op("act", lambda e: e.activation(out=sp_in[0:2, 1, :], in_=ps[0][0:2, 0:512], func=AF.Copy),
             reads=[("ps", 0)], writes=[("sp_ncp",)])
        out_dmas.append(P.op("sp", lambda e: e.dma_start(out=ncp[:, :], in_=sp_in[0:2, 1, :]),
                             reads=[("sp_ncp",)], dma=True))
        for gi in range(4):
            P.op("pe", lambda e, gi=gi: e.transpose(out=ps[1][0:15, gi * 128:(gi + 1) * 128], in_=pext[:, gi, 1:16],
                                                    identity=idf[:, :]),
                 reads=[("pext", gi), ("idf",)], writes=[("ps", 1)])
        P.op("act", lambda e: e.activation(out=sp_in[0:15, 0, :], in_=ps[1][0:15, 0:512], func=AF.Copy),
             reads=[("ps", 1)], writes=[("sp_npp",)])
        out_dmas.append(P.op("sp", lambda e: e.dma_start(out=npp[:, :], in_=sp_in[0:15, 0, :]),
                             reads=[("sp_npp",)], dma=True))
        P.alias([("sp_in",)], [("sp_ncp",), ("sp_npp",)])

    def stq_buf(h, gi):
        if h == 0:
            return vext[:, gi // 2, (gi % 2) * 128:(gi % 2) * 128 + 120]
        base = Bsb2 if gi < 2 else tbuf2
        return base[:, (gi % 2) * 128:(gi % 2) * 128 + 120]
    STQ_IDS = [("stq", h, gi) for h in range(2) for gi in range(4)]

    def sample_state_out_seq():
        stage, fin_ = [], []

        def conv_stage():
            for i in range(4):
                ve = vexts[:, i, :].rearrange("p (s e) -> p s e", s=SSEQ)
                P.op("act", lambda e, ve=ve, i=i: e.activation(
                    out=stgc[:, i, :].rearrange("p (s e) -> p s e", s=SSEQ), in_=ve[:, :, 8:10], func=AF.Copy),
                    reads=[("vexts", i)], writes=[("stgc", i)])

        def conv_fin():
            for i in range(4):
                P.op("pe", lambda e, i=i: e.transpose(out=ps[7][0:32, i * 128:(i + 1) * 128], in_=stgc[:, i, :],
                                                      identity=idf[:, :]),
                     reads=[("stgc", i), ("idf",)], writes=[("ps", 7)])
            P.op("act", lambda e: e.activation(out=sp_in[0:32, 0, :], in_=ps[7][0:32, 0:512], func=AF.Copy),
                 reads=[("ps", 7)], writes=[("sp_in",)])
            out_dmas.append(P.op("sp", lambda e: e.dma_start(out=ncs[:, :], in_=sp_in[0:32, 0, :]),
                                 reads=[("sp_in",)], dma=True))
        stage.append(conv_stage)
        fin_.append(conv_fin)
        for h in range(2):
            def pool_stage(h=h):
                for gi in range(4):
                    pe_ = pexts[:, gi, :].rearrange("p (s e) -> p s e", s=SSEQ)
                    P.op("act", lambda e, pe_=pe_, gi=gi: e.activation(
                        out=stq_buf(h, gi).rearrange("p (s e) -> p s e", s=8),
                        in_=pe_[:, h * 8:(h + 1) * 8, 8:23], func=AF.Copy),
                        reads=[("pexts", gi)], writes=[("stq", h, gi)])

            def pool_fin(h=h):
                bank = 6 if h == 0 else 7
                for gi in range(4):
                    P.op("pe", lambda e, gi=gi: e.transpose(
                        out=ps[bank][0:120, gi * 128:(gi + 1) * 128], in_=stq_buf(h, gi), identity=idf[:, :]),
                        reads=[("stq", h, gi), ("idf",)], writes=[("ps", bank)])
                P.op("act", lambda e: e.activation(out=sp_in[0:120, h, :], in_=ps[bank][0:120, 0:512],
                                                   func=AF.Copy),
                     reads=[("ps", bank)], writes=[("sp_in",)])
                if h == 1:
                    out_dmas.append(P.op("sp", lambda e: e.dma_start(out=nps.rearrange("(h r) c -> r h c", r=120),
                                                                     in_=sp_in[:, :, :]),
                                         reads=[("sp_in",)], dma=True))
            stage.append(pool_stage)
            fin_.append(pool_fin)
        return stage, fin_

    fin_cnt = [0]

    tail_bufs = []

    def enter_tail():
        flat = hT[:, :, :].rearrange("p k t -> p (k t)").bitcast(F32)
        ids = [("tailst", j) for j in range(8)]
        P.alias(ids, [("hT", tt) for tt in range(NT)])
        for j in range(8):
            tail_bufs.append((flat[:, j * 1024:(j + 1) * 1024], ids[j]))

    def final_last(tt):
        k = hp_of[tt]
        if tail_bufs:
            o_ap, oid = tail_bufs.pop(0)
        else:
            c = fin_cnt[0] % 2
            fin_cnt[0] += 1
            o_ap = ost[:, :] if c == 0 else ost2
            oid = ("ost", c)
        P.op("dve", lambda e: e.scalar_tensor_tensor(out=o_ap, in0=xres[:, tt, :], scalar=stat(k, 2),
                                                     in1=gbc[:, :], op0=ALU.mult, op1=ALU.mult),
             reads=[("x", tt, 0), ("x", tt, 1), ("st", k), ("gbc",)], writes=[oid])
        dst = yp[tt * 128:(tt + 1) * 128, :] if tt < 16 else ys[:, :]
        out_dmas.append(P.op("sp", lambda e: e.dma_start(out=dst, in_=o_ap), reads=[oid], dma=True))

    def final_seq(tiles):
        seq = []
        for t in tiles:
            seq.append(lambda t=t: nA1(t))
            seq.append(lambda t=t: nA2(t))
            seq.append(lambda t=t: final_last(t))
        return seq

    def run_ffn(f, pre_hooks, post_fn, block_start_fn, tail_fn=None, post_now=False, defer_tail=False, lead=0,
                last_order=None, after_lead=None):
        items = [(b, g) for b in range(len(BLOCKS)) for g in range(len(FG))]
        if last_order is not None:
            items = items[:-len(FG)] + [(len(BLOCKS) - 1, g) for g in last_order]
        started = set()
        held = []
        prev = None
        pending = []
        for idx, (b, g) in enumerate(items):
            buf = idx % 2
            gu = ffn_GU_ops(f, b, g, buf)
            dd = ffn_D_ops(f, prev[0], prev[1], prev[2]) if prev is not None else []
            extra = list(pre_hooks.get((b, g), [])) + pending
            pending = []
            if idx == 0 and lead:
                for c in extra[:lead]:
                    c()
                extra = extra[lead:]
                if after_lead is not None:
                    after_lead()
            d_nb[0] = 2 if extra else 3
            dd = ffn_D_ops(f, prev[0], prev[1], prev[2]) if prev is not None else []
            if idx == 0 and f == 2:
                interleave(gu[:-1], merge(extra, dd))
                gu[-1]()
            else:
                interleave(gu, merge(extra, dd))
            d_nb[0] = 3
            if b not in started:
                started.add(b)
                block_start_fn(b)
            if prev is not None and prev[0] == len(BLOCKS) - 1:
                if post_now:
                    cl = post_fn(prev[1])
                    if idx == len(items) - 1:
                        for i_ in range(len(cl) // 3):
                            cl[3 * i_]()
                            cl[3 * i_ + 1]()
                        held = [cl[3 * i_ + 2] for i_ in range(len(cl) // 3)]
                    else:
                        for c in cl:
                            c()
                else:
                    pending = post_fn(prev[1])
            prev = (b, g, buf)
        if tail_fn is not None:
            tail_fn()
        if defer_tail:
            for emit in ffn_D_ops(f, prev[0], prev[1], prev[2]):
                emit()
            return pending + post_fn(prev[1])
        if post_now:
            dl = ffn_D_ops(f, prev[0], prev[1], prev[2])
            fl = post_fn(prev[1])
            nt_ = len(FG[prev[1]][2])
            enter_tail()
            assert len(dl) == 2 * nt_ and len(fl) == 3 * nt_ and not pending
            for ti in range(nt_):
                dl[2 * ti]()
                dl[2 * ti + 1]()
                if held:
                    held.pop(0)()
                if ti > 0:
                    fl[3 * (ti - 1) + 2]()
                fl[3 * ti]()
                fl[3 * ti + 1]()
            for c in held:
                c()
            fl[3 * (nt_ - 1) + 2]()
            return []
        for emit in merge(ffn_D_ops(f, prev[0], prev[1], prev[2]), pending):
            emit()
        for emit in post_fn(prev[1]):
            emit()
        return []

    norm_tiles(FG[0][2])
    load_first_block(2, len(BLOCKS[0]), False)

    def after_lead1():
        load_first_block(len(BLOCKS[0]), len(BLOCKS[0]), True)
        load_x_rest()
    pre1 = {}
    for g in range(1, len(FG)):
        pre1[(0, g - 1)] = norm_seq(FG[g][2])
    pre1[(2, 1)] = sample_state_prep_pool_seq()

    state = {}

    def ffn1_block_start(b):
        if b == 0:
            load_after[0] = (P.byeng["pe"][-1],)
            load_ffn_block(1, 1)
            load_after[0] = ()
            return
        if b + 1 < len(BLOCKS):
            load_ffn_block(1, b + 1)
        else:
            load_mix_units(["C", "u", "B"])

    def ffn1_post(g):
        if not state.get("gbc_mix"):
            load_gbc(norm_mix)
            state["gbc_mix"] = True
        return norm_seq(FG[g][2])

    late_norm = run_ffn(1, pre1, ffn1_post, ffn1_block_start, tail_fn=lambda: load_mix_units(["p", "o0"]),
                        defer_tail=True, lead=12, after_lead=after_lead1)

    def ffn2_slots(b):
        return (3, 0, 1) if b % 2 == 0 else (2, 4, 5)

    load_mix_units(["o1"])
    P.alias(MIX_SCR_IDS, FFN_SCR_IDS)
    tb_banks[0] = [7]

    def mixer_phase2_setup():
        nhp[0] = 3
        P.alias([("sc_in",)], [("hp", 3)])
        P.op("sp", lambda e: e.dma_start(out=sc_in[:, :], in_=sc_d[:, :]), writes=[("sc_in",)], dma=True)
        load_gbc(norm_ffn2)

    LT = FG[3][2] + FG[4][2]
    assert len(late_norm) == 4 * len(LT) and len(LT) == 5
    LATE_SCHED = {
        0: [(nA1, LT[0]), (nA2, LT[0]), (nA1, LT[1]), (nA2, LT[1])],
        1: [(nA1, LT[2]), (nA2, LT[2]), (nA3, LT[0])],
        2: [(nA1, LT[3]), (nA2, LT[3]), (nA3, LT[1]), (norm_B, LT[0])],
        3: [(nA3, LT[2]), (norm_B, LT[1])],
        4: [(nA1, LT[4]), (nA2, LT[4]), (nA3, LT[3]), (norm_B, LT[2])],
        5: [(nA3, LT[4]), (norm_B, LT[3])],
        6: [(norm_B, LT[4])],
    }
    trickle = {}
    for t in range(4):
        trickle.setdefault((t + 2, 1), []).append(lambda t=t: nA1(t))
        trickle.setdefault((t + 2, 2), []).append(lambda t=t: nA2(t))
        trickle.setdefault((t + 3, 1), []).append(lambda t=t: nA3(t))
        trickle.setdefault((t + 4, 1), []).append(lambda t=t: norm_B(t))
    tb_banks[0] = [5, 6]
    for g in range(9):
        if g == 2:
            tb_banks[0] = [7]
            mixer_phase2_setup()
        if g >= 2:
            mix_pool_mm(g - 1)
        if g == 4:
            sample_state_prep_conv()
        if g == 8:
            mix_conv(8, 0)
            mix_pool_win(8)
            load_ffn_block(2, 0, parts="g", slots=ffn2_slots(0))
            for i in range(1, 4):
                mix_conv(8, i)
            load_ffn_block(2, 0, parts="u", slots=ffn2_slots(0))
            P.alias([("hp", 3)], [("sc_in",)])
            nhp[0] = 4
            for t_ in FG[1][2]:
                nA1(t_)
                nA2(t_)
            load_ffn_block(2, 0, parts="d", slots=ffn2_slots(0))
            mix_wout(7)
            mix_pool_fin(8)
            prompt_state_out()
            mix_pool_mm(8)
            for t_ in FG[1][2]:
                nA3(t_)
            mix_wout(8)
            continue
        for i in range(4):
            if g < 2:
                for fn_, t_ in LATE_SCHED.get(g * 4 + i, []):
                    fn_(t_)
            mix_conv(g, i)
            if g == 0:
                if i == 2:
                    mix_pool_win(0)
            elif g == 1:
                if i == 0:
                    mix_pool_fin(0)
                if i == 1:
                    mix_pool_mm(0)
                if i == 2:
                    mix_pool_win(1)
                    mix_wout(0)
                    mix_pool_fin(1)
            else:
                if i == 0:
                    mix_pool_win(g)
                if i == 1:
                    mix_wout(g - 1)
                    mix_pool_fin(g)
            for c in trickle.get((g, i), []):
                c()
    P.alias(FFN_SCR_IDS + [("ost", 1)], MIX_SCR_IDS)
    P.alias([("ost", 0)], [("F", gi_) for gi_ in range(4)])
    P.alias(STQ_IDS, [("vext", 0), ("vext", 1), ("Bsb2",), ("tbuf2",)])
    tb_banks[0] = [7, 6]
    load_ffn_block(2, 1, parts="ud", slots=ffn2_slots(1))
    load_ffn_block(2, 1, parts="g", slots=ffn2_slots(1))

    def ffn2_block_start(b):
        if 1 <= b and b + 1 < len(BLOCKS):
            load_ffn_block(2, b + 1, slots=ffn2_slots(b + 1))

    def ffn2_post(g):
        if not state.get("gbc_fin"):
            load_gbc(norm_final)
            state["gbc_fin"] = True
        return final_seq(FG[g][2])

    pre2 = {}
    for g in range(1, len(FG)):
        pre2[(0, g - 1)] = norm_seq(FG[g][2])
    so_stage, so_fin = sample_state_out_seq()
    pre2[(0, 0)] = so_stage + [lambda t=t, i=i: norm_B(t, bank=4 + i % 2) for i, t in enumerate(FG[1][2])] + so_fin
    run_ffn(2, pre2, ffn2_post, ffn2_block_start, post_now=True, last_order=[4, 0, 1, 2, 3])

    fin = P.op("sp", lambda e: None)
    fin.deps = set(out_dmas)

    from contextlib import ExitStack
    with ExitStack() as es:
        sems = {e: es.enter_context(nc.semaphore("sem_" + e)) for e in Prog.ENGS}
        dma_sems = {
            "pool": [es.enter_context(nc.semaphore("dq_pool%d" % i)) for i in range(8)],
            "sp": [es.enter_context(nc.semaphore("dq_sp%d" % i)) for i in range(24)],
            "act": [], "pe": [], "dve": [],
        }
        P.finalize(nc, sems, dma_sems)
        block = es.enter_context(nc.Block())

        @block.sync
        def _(e):
            P.emit_engine("sp", e)

        @block.gpsimd
        def _(e):
            P.emit_engine("pool", e)

        @block.scalar
        def _(e):
            P.emit_engine("act", e)

        @block.vector
        def _(e):
            P.emit_engine("dve", e)

        @block.tensor
        def _(e):
            P.emit_engine("pe", e)
    return nc


_NC_CACHE = {}


def kernel(x_prompt, x_sample, state_conv, state_pool,
           norm_ffn1, ffn1_gate, ffn1_up, ffn1_down,
           norm_mix, w_in, conv_w, pool_w, pool_scale, w_out,
           norm_ffn2, ffn2_gate, ffn2_up, ffn2_down, norm_final):
    f32 = lambda a: np.ascontiguousarray(np.asarray(a, dtype=np.float32))
    x_prompt, x_sample, state_conv, state_pool = map(f32, (x_prompt, x_sample, state_conv, state_pool))
    shared = dict(norm_ffn1=f32(norm_ffn1), ffn1_gate=f32(ffn1_gate), ffn1_up=f32(ffn1_up), ffn1_down=f32(ffn1_down),
                  norm_mix=f32(norm_mix), w_in=f32(w_in), conv_w=f32(conv_w), pool_w=f32(pool_w),
                  pool_scale=f32(pool_scale), w_out=f32(w_out), norm_ffn2=f32(norm_ffn2),
                  ffn2_gate=f32(ffn2_gate), ffn2_up=f32(ffn2_up), ffn2_down=f32(ffn2_down),
                  norm_final=f32(norm_final))
    in_maps = []
    for c in range(NCORES):
        m = dict(shared)
        m["xp"] = x_prompt[c]
        m["xs"] = x_sample[c * SSEQ:(c + 1) * SSEQ].reshape(TS, D)
        m["sc"] = state_conv[c * SSEQ:(c + 1) * SSEQ].reshape(SSEQ * 2, CONVD)
        m["spl"] = state_pool[c * SSEQ:(c + 1) * SSEQ].reshape(SSEQ * 15, POOLD)
        in_maps.append(m)
    if "nc" not in _NC_CACHE:
        _NC_CACHE["nc"] = build_nc()
    nc = _NC_CACHE["nc"]
    res = run_bass_kernel_spmd(nc, in_maps, core_ids=list(range(NCORES)))
    r = res.results
    y_prompt = np.stack([r[c]["yp"] for c in range(NCORES)], axis=0)
    y_sample = np.concatenate([r[c]["ys"].reshape(SSEQ, ST, D) for c in range(NCORES)], axis=0)
    ncp = np.stack([r[c]["ncp"] for c in range(NCORES)], axis=0)
    npp = np.stack([r[c]["npp"] for c in range(NCORES)], axis=0)
    ncs = np.concatenate([r[c]["ncs"].reshape(SSEQ, 2, CONVD) for c in range(NCORES)], axis=0)
    nps = np.concatenate([r[c]["nps"].reshape(SSEQ, 15, POOLD) for c in range(NCORES)], axis=0)
    return (y_prompt.astype(np.float32), y_sample.astype(np.float32), ncp.astype(np.float32),
            npp.astype(np.float32), ncs.astype(np.float32), nps.astype(np.float32))
```
